# Optimizing a Trainium2 kernel written in Bass

```python
import jax, jax.numpy as jnp
from jax import lax
import numpy as np

D_MODEL = 1024
BATCH = 2
SEQ = 16384
DEPTH = 1
DEC_BATCH = 32
DEC_SEQ = 2048
PAST_LEN = 128

N_HEADS = 8
N_KV_HEADS = 2
HEAD_DIM = 128
GROUP = N_HEADS // N_KV_HEADS
WINDOW = 128
BLOCK = 128
SPAN = BLOCK + 2 * WINDOW
Q_W = N_HEADS * HEAD_DIM
KV_W = N_KV_HEADS * HEAD_DIM
NEG_INF = -1e30
D_RNN = 1024
N_RNN_BLOCKS = 8
RNN_BLOCK = D_RNN // N_RNN_BLOCKS
CONV_WIDTH = 4
CONV_LEFT = 2
LRU_C = 8.0
D_FF = 2816
EPS = 1e-6
D_IN = Q_W + 2 * KV_W + 2 * D_RNN + 2 * D_MODEL
SPLIT_POINTS = (Q_W, Q_W + KV_W, Q_W + 2 * KV_W, Q_W + 2 * KV_W + D_RNN,
                Q_W + 2 * KV_W + 2 * D_RNN, Q_W + 2 * KV_W + 2 * D_RNN + D_MODEL)

kernel_name = "hybrid_bidir_local_gqa_rglru_macaron"


def rmsnorm(x, g):
    xf = x.astype(jnp.float32)
    y = xf * lax.rsqrt(jnp.mean(xf * xf, axis=-1, keepdims=True) + EPS)
    return (y * g.astype(jnp.float32)).astype(x.dtype)


def swiglu(x, w_gate, w_up, w_down):
    return (jax.nn.silu(x @ w_gate) * (x @ w_up)) @ w_down


def alibi_slopes():
    return 2.0 ** (-8.0 * jnp.arange(1, N_HEADS + 1, dtype=jnp.float32) / N_HEADS)


def local_attention(q, k, v, sink):
    B, S, _ = q.shape
    nblk = S // BLOCK
    qb = jnp.moveaxis(q.reshape(B, nblk, BLOCK, N_KV_HEADS, GROUP, HEAD_DIM), 1, 0)
    pad = ((0, 0), (WINDOW, WINDOW), (0, 0), (0, 0))
    kp = jnp.pad(k.reshape(B, S, N_KV_HEADS, HEAD_DIM), pad)
    vp = jnp.pad(v.reshape(B, S, N_KV_HEADS, HEAD_DIM), pad)
    qi = jnp.arange(BLOCK)[:, None]
    kc = jnp.arange(SPAN)[None, :]
    dist = jnp.abs(qi + WINDOW - kc)
    band = dist <= WINDOW
    slopes = alibi_slopes().reshape(N_KV_HEADS, GROUP)
    bias = -slopes[:, :, None, None] * dist.astype(jnp.float32)[None, None]
    sink_f = sink.astype(jnp.float32).reshape(N_KV_HEADS, GROUP)[:, :, None, None]
    scale = HEAD_DIM ** -0.5

    def one_block(args):
        q_blk, j = args
        k_blk = lax.dynamic_slice_in_dim(kp, j * BLOCK, SPAN, axis=1)
        v_blk = lax.dynamic_slice_in_dim(vp, j * BLOCK, SPAN, axis=1)
        s = jnp.einsum("bqkgd,bskd->bkgqs", q_blk, k_blk,
                       preferred_element_type=jnp.float32) * scale + bias
        key_pos = j * BLOCK - WINDOW + jnp.arange(SPAN)
        valid = band & ((key_pos >= 0) & (key_pos < S))[None, :]
        s = jnp.where(valid, s, NEG_INF)
        m = jnp.maximum(jnp.max(s, axis=-1, keepdims=True), sink_f)
        p = jnp.exp(s - m)
        denom = jnp.sum(p, axis=-1, keepdims=True) + jnp.exp(sink_f - m)
        return jnp.einsum("bkgqs,bskd->bqkgd", (p / denom).astype(v_blk.dtype), v_blk)

    out = lax.map(one_block, (qb, jnp.arange(nblk)))
    return jnp.moveaxis(out, 0, 1).reshape(B, S, Q_W)


def centred_conv(x, w, b):
    S = x.shape[1]
    xp = jnp.pad(x, ((0, 0), (CONV_LEFT, CONV_WIDTH - 1 - CONV_LEFT), (0, 0)))
    out = b + xp[:, 0:S] * w[0]
    for tap in range(1, CONV_WIDTH):
        out = out + xp[:, tap:tap + S] * w[tap]
    return out


def _lin_combine(e1, e2):
    a1, b1 = e1
    a2, b2 = e2
    return a1 * a2, a2 * b1 + b2


def rglru(x, w_a, b_a, w_i, b_i, lam):
    B, S, _ = x.shape
    xb = x.reshape(B, S, N_RNN_BLOCKS, RNN_BLOCK)
    r = jax.nn.sigmoid(jnp.einsum("bsnc,ncd->bsnd", xb, w_a).reshape(B, S, D_RNN) + b_a)
    ig = jax.nn.sigmoid(jnp.einsum("bsnc,ncd->bsnd", xb, w_i).reshape(B, S, D_RNN) + b_i)
    log_a = -LRU_C * r.astype(jnp.float32) * jax.nn.softplus(-lam.astype(jnp.float32))
    a = jnp.exp(log_a)
    u = jnp.sqrt(-jnp.expm1(2.0 * log_a)) * (ig * x).astype(jnp.float32)
    _, h = lax.associative_scan(_lin_combine, (a, u), axis=1)
    return h


def token_mixer(h, w_in, attn_sink, conv_w, conv_b,
                wa_f, ba_f, wi_f, bi_f, lam_f, wa_b, ba_b, wi_b, bi_b, lam_b,
                w_attn_o, w_rnn_o, w_out):
    z = h @ w_in
    q, k, v, xr, yr, g_attn, g_rnn = jnp.split(z, SPLIT_POINTS, axis=-1)
    attn = local_attention(q, k, v, attn_sink) @ w_attn_o
    xc = centred_conv(xr, conv_w, conv_b)
    h_fwd = rglru(xc, wa_f, ba_f, wi_f, bi_f, lam_f)
    h_bwd = jnp.flip(rglru(jnp.flip(xc, axis=1), wa_b, ba_b, wi_b, bi_b, lam_b), axis=1)
    hr = (h_fwd + h_bwd).astype(h.dtype)
    rnn = (hr * jax.nn.gelu(yr)) @ w_rnn_o
    merged = jax.nn.sigmoid(g_attn) * attn + jax.nn.sigmoid(g_rnn) * rnn
    return merged @ w_out


def trunk(x, lw, norm_final):
    (norm_ffn1, ffn1_gate, ffn1_up, ffn1_down, norm_mix, w_in, attn_sink, conv_w, conv_b,
     wa_f, ba_f, wi_f, bi_f, lam_f, wa_b, ba_b, wi_b, bi_b, lam_b,
     w_attn_o, w_rnn_o, w_out, norm_ffn2, ffn2_gate, ffn2_up, ffn2_down) = lw
    for l in range(DEPTH):
        x = x + 0.5 * swiglu(rmsnorm(x, norm_ffn1[l]), ffn1_gate[l], ffn1_up[l], ffn1_down[l])
        x = x + token_mixer(rmsnorm(x, norm_mix[l]), w_in[l], attn_sink[l], conv_w[l], conv_b[l],
                            wa_f[l], ba_f[l], wi_f[l], bi_f[l], lam_f[l],
                            wa_b[l], ba_b[l], wi_b[l], bi_b[l], lam_b[l],
                            w_attn_o[l], w_rnn_o[l], w_out[l])
        x = x + 0.5 * swiglu(rmsnorm(x, norm_ffn2[l]), ffn2_gate[l], ffn2_up[l], ffn2_down[l])
    return rmsnorm(x, norm_final)


def setup_inputs(seed: int = 0) -> dict:
    key = jax.random.key(seed)
    ks = iter(jax.random.split(key, 40))
    f32 = jnp.float32

    def nrm(shape, scale):
        return jax.random.normal(next(ks), shape, f32) * scale

    def gain(shape):
        return 1.0 + 0.05 * jax.random.normal(next(ks), shape, f32)

    def lam_init():
        a0 = jax.random.uniform(next(ks), (DEPTH, D_RNN), f32, 0.9, 0.999)
        p = a0 ** (1.0 / LRU_C)
        return jnp.log(p) - jnp.log1p(-p)

    L = DEPTH
    return {
        "x_prompt": jax.random.normal(next(ks), (BATCH, SEQ, D_MODEL), f32),
        "x_sample": jax.random.normal(next(ks), (DEC_BATCH, DEC_SEQ, D_MODEL), f32),
        "norm_ffn1": gain((L, D_MODEL)),
        "ffn1_gate": nrm((L, D_MODEL, D_FF), D_MODEL ** -0.5),
        "ffn1_up": nrm((L, D_MODEL, D_FF), D_MODEL ** -0.5),
        "ffn1_down": nrm((L, D_FF, D_MODEL), D_FF ** -0.5),
        "norm_mix": gain((L, D_MODEL)),
        "w_in": nrm((L, D_MODEL, D_IN), D_MODEL ** -0.5),
        "attn_sink": nrm((L, N_HEADS), 0.5),
        "conv_w": nrm((L, CONV_WIDTH, D_RNN), CONV_WIDTH ** -0.5),
        "conv_b": nrm((L, D_RNN), 0.02),
        "lru_wa_f": nrm((L, N_RNN_BLOCKS, RNN_BLOCK, RNN_BLOCK), RNN_BLOCK ** -0.5),
        "lru_ba_f": nrm((L, D_RNN), 0.1),
        "lru_wi_f": nrm((L, N_RNN_BLOCKS, RNN_BLOCK, RNN_BLOCK), RNN_BLOCK ** -0.5),
        "lru_bi_f": nrm((L, D_RNN), 0.1),
        "lru_lam_f": lam_init(),
        "lru_wa_b": nrm((L, N_RNN_BLOCKS, RNN_BLOCK, RNN_BLOCK), RNN_BLOCK ** -0.5),
        "lru_ba_b": nrm((L, D_RNN), 0.1),
        "lru_wi_b": nrm((L, N_RNN_BLOCKS, RNN_BLOCK, RNN_BLOCK), RNN_BLOCK ** -0.5),
        "lru_bi_b": nrm((L, D_RNN), 0.1),
        "lru_lam_b": lam_init(),
        "w_attn_o": nrm((L, Q_W, D_MODEL), Q_W ** -0.5),
        "w_rnn_o": nrm((L, D_RNN, D_MODEL), D_RNN ** -0.5),
        "w_out": nrm((L, D_MODEL, D_MODEL), D_MODEL ** -0.5),
        "norm_ffn2": gain((L, D_MODEL)),
        "ffn2_gate": nrm((L, D_MODEL, D_FF), D_MODEL ** -0.5),
        "ffn2_up": nrm((L, D_MODEL, D_FF), D_MODEL ** -0.5),
        "ffn2_down": nrm((L, D_FF, D_MODEL), D_FF ** -0.5),
        "norm_final": gain((D_MODEL,)),
    }


def reference(x_prompt, x_sample, norm_ffn1, ffn1_gate, ffn1_up, ffn1_down, norm_mix, w_in,
              attn_sink, conv_w, conv_b, lru_wa_f, lru_ba_f, lru_wi_f, lru_bi_f, lru_lam_f,
              lru_wa_b, lru_ba_b, lru_wi_b, lru_bi_b, lru_lam_b, w_attn_o, w_rnn_o, w_out,
              norm_ffn2, ffn2_gate, ffn2_up, ffn2_down, norm_final):
    lw = (norm_ffn1, ffn1_gate, ffn1_up, ffn1_down, norm_mix, w_in, attn_sink, conv_w, conv_b,
          lru_wa_f, lru_ba_f, lru_wi_f, lru_bi_f, lru_lam_f,
          lru_wa_b, lru_ba_b, lru_wi_b, lru_bi_b, lru_lam_b,
          w_attn_o, w_rnn_o, w_out, norm_ffn2, ffn2_gate, ffn2_up, ffn2_down)
    y_prompt = trunk(x_prompt, lw, norm_final)
    y_sample = trunk(x_sample, lw, norm_final)
    return (y_prompt, y_sample)
```

```python
import numpy as np
from collections import deque
from contextlib import ExitStack

import concourse.bass as bass
import concourse.mybir as mybir
from concourse.bass_utils import run_bass_kernel_spmd

F32 = mybir.dt.float32
BF16 = mybir.dt.bfloat16
AF = mybir.ActivationFunctionType
ALU = mybir.AluOpType
AX = mybir.AxisListType

D = 1024
KC = 8
DFF = 2816
FC = 22
T = 512
EPS = 1e-6
NCORES = 8
NT_FULL = 16384
PAD = 128
SCALE = 128 ** -0.5

PC_G1, PC_GM, PC_G2, PC_GF = 0, 8, 16, 24
PC_CW = 32
PC_CB = 64
PC_BAF, PC_BIF, PC_LAMF = 72, 80, 88
PC_BAB, PC_BIB, PC_LAMB = 96, 104, 112
PC_SINK = 120
PC_SLOPE = 128
PC_FLAG = 136


class Buf:
    __slots__ = ("name", "w", "rs", "ro")

    def __init__(self, name, ro=False):
        self.name = name
        self.w = None
        self.rs = {}
        self.ro = ro


class Op:
    __slots__ = ("eng", "fn", "deps", "is_dma", "signal", "sigval", "dsem", "dval", "seq", "small")


ENGS = ("pe", "act", "dve", "pool", "sp")
NRING = 8


class Prog:
    def __init__(self, nc, es):
        self.nc = nc
        self.h = {"pe": nc.tensor, "act": nc.scalar, "dve": nc.vector, "pool": nc.gpsimd, "sp": nc.sync}
        self.sems = []
        self.esem = {}
        for e in ENGS:
            self.esem[e] = len(self.sems)
            self.sems.append(es.enter_context(nc.semaphore("s_" + e)))
        self.ring = {}
        for q in ("sp", "pool", "act"):
            self.ring[q] = []
            for i in range(NRING):
                self.ring[q].append(len(self.sems))
                self.sems.append(es.enter_context(nc.semaphore("d_%s%d" % (q, i))))
        self.cnt = {e: 0 for e in ENGS}
        self.ringcnt = {q: 0 for q in self.ring}
        self.waited = {e: {} for e in ENGS}
        self.ops = {e: [] for e in ENGS}
        self.last = {e: None for e in ENGS}
        self.dmas = []
        self.seq = 0
        self.nins = 0

    def add(self, eng, fn, R=(), W=(), dma=False, deps=(), small=False, acc=False):
        op = Op()
        op.small = small
        wprev = set(id(b.w) for b in W if b.w is not None) if acc else ()
        op.eng = eng
        op.fn = fn
        op.is_dma = dma
        op.signal = False
        op.sigval = 0
        op.dsem = -1
        op.dval = 0
        op.seq = self.seq
        self.seq += 1
        dl = {}
        for d in deps:
            dl[d.seq] = d
        for b in R:
            if b.w is not None:
                dl[b.w.seq] = b.w
        for b in W:
            if b.w is not None:
                dl[b.w.seq] = b.w
            for r in b.rs.values():
                dl[r.seq] = r
        for b in W:
            b.w = op
            b.rs = {}
        if dma:
            for b in R:
                if not b.ro:
                    b.rs[("dma", op.seq)] = op
        else:
            for b in R:
                if not b.ro:
                    b.rs[eng] = op
        dd = []
        for d in dl.values():
            if d is op:
                continue
            if acc and (not d.is_dma) and d.eng == eng and id(d) in wprev:
                continue
            dd.append(d)
            if not d.is_dma:
                d.signal = True
        op.deps = dd
        self.ops[eng].append(op)
        if dma:
            self.dmas.append(op)
        else:
            self.last[eng] = op
        return op

    def mm(self, out, lhsT, rhs, start, stop, R, W, acc=None):
        return self.add("pe", lambda h: h.matmul(out, lhsT=lhsT, rhs=rhs, start=start, stop=stop),
                        R=R, W=W, acc=((not start) if acc is None else acc))

    def flush(self):
        for e in ENGS:
            for op in self.ops[e]:
                if op.is_dma:
                    i = self.ringcnt[e]
                    self.ringcnt[e] += 1
                    op.dsem = self.ring[e][i % NRING]
                    op.dval = 16 * (i // NRING + 1)
                elif op.signal:
                    self.cnt[e] += 1
                    op.sigval = self.cnt[e]
        for e in ENGS:
            h = self.h[e]
            waited = self.waited[e]
            for op in self.ops[e]:
                need = {}
                for d in op.deps:
                    if d.is_dma:
                        s, v = d.dsem, d.dval
                    else:
                        s, v = self.esem[d.eng], d.sigval
                    if need.get(s, 0) < v:
                        need[s] = v
                if op.is_dma and op.dval > 16:
                    if need.get(op.dsem, 0) < op.dval - 16:
                        need[op.dsem] = op.dval - 16
                for s, v in need.items():
                    if waited.get(s, 0) < v:
                        h.wait_ge(self.sems[s], v)
                        waited[s] = v
                        self.nins += 1
                ins = op.fn(h)
                self.nins += 1
                if op.is_dma:
                    ins.then_inc(self.sems[op.dsem], 16)
                elif op.signal:
                    ins.then_inc(self.sems[self.esem[e]], 1)
            self.ops[e] = []

    def barrier(self):
        deps = [self.last[e] for e in ENGS if self.last[e] is not None] + list(self.dmas)
        join = self.add("sp", lambda h: h.nop(), deps=deps)
        for e in ENGS:
            if e != "sp":
                self.add(e, lambda h: h.nop(), deps=[join])
        self.dmas = []
        self.flush()


def pieces(w, step=512):
    out = []
    c = 0
    while c < w:
        out.append((c, min(step, w - c)))
        c += step
    return out


class Builder:
    def __init__(self, NT, debug=False):
        self.NT = NT
        self.NTP = NT + 2 * PAD
        self.n512 = NT // T
        self.debug = debug
        self.NPAR = PC_FLAG + 2 * self.n512
        self._uid = 0
        nc = self.nc = bass.Bass("TRN2", target_bir_lowering=False)

        def din(name, shape, dt=F32):
            return nc.dram_tensor(name, shape, dt, kind="ExternalInput").ap()

        def dscr(name, shape, dt=F32):
            if debug:
                return nc.dram_tensor(name, shape, dt, kind="ExternalOutput").ap()
            return nc.dram_tensor(name, shape, dt).ap()

        self.xT = din("xT", [D, NT])
        self.par_d = din("par", [128, self.NPAR])
        self.cst_d = din("cst", [128, 512])
        self.w = {}
        for nm, shp in (("ffn1_gate", [D, DFF]), ("ffn1_up", [D, DFF]), ("ffn1_down", [DFF, D]),
                        ("w_in", [D, 5632]),
                        ("lru_wa_f", [1024, 128]), ("lru_wi_f", [1024, 128]),
                        ("lru_wa_b", [1024, 128]), ("lru_wi_b", [1024, 128]),
                        ("w_attn_o", [D, D]), ("w_rnn_o", [D, D]), ("w_out", [D, D]),
                        ("ffn2_gate", [D, DFF]), ("ffn2_up", [D, DFF]), ("ffn2_down", [DFF, D])):
            self.w[nm] = din(nm, shp)
        self.yT = nc.dram_tensor("yT", [D, NT], F32, kind="ExternalOutput").ap()
        self.x1_s = dscr("x1_s", [D, self.NTP])
        self.xr_s = dscr("xr_s", [D, self.NTP])
        self.gy_s = dscr("gy_s", [D, NT])
        self.xcb_s = dscr("xcb_s", [D, NT], BF16)
        self.hb_s = dscr("hb_s", [D, NT])
        self.hy_s = dscr("hy_s", [D, NT], BF16)
        self.ot_s = dscr("ot_s", [D, NT], BF16)
        self.x2_s = dscr("x2_s", [D, NT])
        if debug:
            self.dbg = nc.dram_tensor("dbg", [8, 128, 2048], F32, kind="ExternalOutput").ap()

    def _sbt(self, name, shape, dt):
        self._uid += 1
        return self.nc.sbuf_tensor("sb%d_%s" % (self._uid, name), shape, dt)

    def _pst(self, name, shape, dt):
        self._uid += 1
        return self.nc.psum_tensor("pp%d_%s" % (self._uid, name), shape, dt)

    def dview(self, ap, c0, w):
        return ap[:, c0:c0 + w].rearrange("(k p) t -> p k t", p=128)

    def load_w(self, P, sb, name, src, kc, ncols, c0=0):
        bufs = []
        for k in range(kc):
            bl = []
            for (cc, cw) in pieces(ncols, 1408):
                b = Buf("%s%d_%d" % (name, k, cc), ro=True)
                bl.append(b)
                P.add("pool", (lambda h, k=k, cc=cc, cw=cw: h.dma_start(
                    out=sb[:, k, cc:cc + cw], in_=src[k * 128:(k + 1) * 128, c0 + cc:c0 + cc + cw])),
                    W=[b], dma=True)
            bufs.append(bl)
        return bufs

    def build(self, phases=("p0", "p1", "p2", "p3", "p4", "p5a", "p5b", "p6")):
        nc = self.nc
        with ExitStack() as es:
            P = self.P = Prog(nc, es)
            self.par = es.enter_context(self._sbt("par", [128, self.NPAR], F32))
            self.ones_s = es.enter_context(self._sbt("ones_s", [128, 128], BF16))
            self.b_par = Buf("par", ro=True)
            self.b_ones = Buf("ones", ro=True)
            par, ones_s = self.par, self.ones_s
            P.add("sp", lambda h: h.dma_start(out=par[:], in_=self.par_d[:]), W=[self.b_par], dma=True)
            P.add("dve", lambda h: h.memset(ones_s[:], 1.0 / D), W=[self.b_ones])
            self.epsc = es.enter_context(self._sbt("epsc", [128, 1], F32))
            epsc = self.epsc
            P.add("dve", lambda h: h.memset(epsc[:], EPS), W=[self.b_ones])
            if "p0" in phases:
                self.phase0()
            if "p1" in phases:
                self.ffn_phase(1)
            if "p2" in phases:
                self.phase2()
            if "p3" in phases:
                self.rnn_phase(fwd=False)
            if "p4" in phases:
                self.rnn_phase(fwd=True)
            if "p5a" in phases:
                self.phase5a()
            if "p5b" in phases:
                self.phase5b()
            if "p6" in phases:
                self.ffn_phase(2)
            P.barrier()
        return nc

    def pcol(self, c):
        return self.par[:, c:c + 1]

    def phase0(self):
        nc, P = self.nc, self.P
        with ExitStack() as es:
            z = es.enter_context(self._sbt("zpad", [128, KC, PAD], F32))
            bz = Buf("z")
            P.add("dve", lambda h: h.memset(z[:], 0.0), W=[bz])
            for scr in (self.x1_s, self.xr_s):
                for c0 in (0, PAD + self.NT):
                    P.add("sp", (lambda h, scr=scr, c0=c0: h.dma_start(out=self.dview(scr, c0, PAD), in_=z[:])),
                          R=[bz], dma=True)
            P.barrier()

    def norm_full(self, P, xt, bx, W, gbase, sq, bsq, ps_list, rstd, brstd, hout, bh, stage=None):
        ones_s = self.ones_s
        if stage in (None, "sq"):
            for k in range(KC):
                P.add("act", (lambda h, k=k: h.activation(out=sq[:, k, 0:W], in_=xt[:, k, 0:W], func=AF.Square)),
                      R=[bx], W=[bsq], acc=(k > 0))
        if stage in (None, "ms"):
            for pi, (c0, cw) in enumerate(pieces(W)):
                ps, bps = ps_list[pi % len(ps_list)]
                for k in range(KC):
                    P.mm(ps[:, 0:cw], ones_s[:], sq[:, k, c0:c0 + cw], (k == 0), (k == KC - 1),
                         R=[bsq, self.b_ones], W=[bps])
                P.add("act", (lambda h, c0=c0, cw=cw, ps=ps: h.activation(
                    out=rstd[:, c0:c0 + cw], in_=ps[:, 0:cw], func=AF.Ln, bias=self.epsc[:], scale=1.0)),
                    R=[bps, self.b_ones], W=[brstd])
                P.add("act", (lambda h, c0=c0, cw=cw: h.activation(
                    out=rstd[:, c0:c0 + cw], in_=rstd[:, c0:c0 + cw], func=AF.Exp, scale=-0.5)),
                    R=[brstd], W=[brstd])
        if stage is None:
            k0, k1 = 0, KC
        elif isinstance(stage, tuple):
            k0, k1 = stage[1], stage[2]
        else:
            return
        for k in range(k0, k1):
            P.add("dve", (lambda h, k=k: h.scalar_tensor_tensor(
                out=hout[:, k, 0:W], in0=xt[:, k, 0:W], scalar=self.pcol(gbase + k), in1=rstd[:, 0:W],
                op0=ALU.mult, op1=ALU.mult)), R=[bx, brstd, self.b_par, bsq], W=[bh], acc=(k > 0))

    def ffn_phase(self, which):
        nc, P = self.nc, self.P
        final = (which == 2)
        if which == 1:
            wg_d, wu_d, wd_d = self.w["ffn1_gate"], self.w["ffn1_up"], self.w["ffn1_down"]
            gbase = PC_G1
            src, src_off = self.xT, 0
            dst, dst_off = self.x1_s, PAD
        else:
            wg_d, wu_d, wd_d = self.w["ffn2_gate"], self.w["ffn2_up"], self.w["ffn2_down"]
            gbase = PC_G2
            src, src_off = self.x2_s, 0
            dst, dst_off = self.yT, 0
        ntiles = self.n512
        with ExitStack() as es:
            wg = es.enter_context(self._sbt("wg", [128, KC, DFF], BF16))
            wu = es.enter_context(self._sbt("wu", [128, KC, DFF], BF16))
            wd = es.enter_context(self._sbt("wd", [128, FC, D], BF16))
            xts = [es.enter_context(self._sbt("xt%d" % i, [128, KC, T], F32)) for i in range(2)]
            xn = es.enter_context(self._sbt("xn", [128, KC, T], BF16))
            hid = es.enter_context(self._sbt("hid", [128, FC, T], BF16))
            sgs = [es.enter_context(self._sbt("sg%d" % i, [128, T], F32)) for i in range(2)]
            sqc = [es.enter_context(self._sbt("sqc%d" % i, [128, T], BF16)) for i in range(2)]
            sqf = [es.enter_context(self._sbt("sqf%d" % i, [128, T], BF16)) for i in range(2 if final else 0)]
            b_sqf = [Buf("sqf0"), Buf("sqf1")]
            sqf_ctr = [0]
            rstd = es.enter_context(self._sbt("rstd", [128, T], F32))
            rstdF = es.enter_context(self._sbt("rstdF", [128, T], F32))
            psum = es.enter_context(self._pst("psf", [128, 8, T], F32))
            b_xt = [[Buf("xt%d_%d" % (i, k)) for k in range(KC)] for i in range(2)]
            b_xn = [Buf("xn%d" % k) for k in range(KC)]
            b_hid = [Buf("hid%d" % j) for j in range(FC)]
            b_sg = [Buf("sg0"), Buf("sg1")]
            b_sqc = [Buf("sqc0"), Buf("sqc1")]
            b_rstd, b_rstdF = Buf("rstd"), Buf("rstdF")
            b_ps = [Buf("ps%d" % i) for i in range(8)]
            PS_G, PS_U, PS_Y, PS_MS, PS_MF = (0, 1), (2, 3), (4, 5), 6, 7

            b_wg = self.load_w(P, wg, "wg", wg_d, KC, DFF)
            b_wu = self.load_w(P, wu, "wu", wu_d, KC, DFF)
            b_wd = self.load_w(P, wd, "wd", wd_d, FC, D)

            hooks = deque()

            def group_done():
                if hooks:
                    hooks.popleft()()

            def load(i):
                xt = xts[i % 2]
                P.add("sp", lambda h: h.dma_start(out=xt[:], in_=self.dview(src, src_off + i * T, T)),
                      W=b_xt[i % 2], dma=True)

            sq_ctr = [0]

            def norm(i):
                xt = xts[i % 2]
                bx = b_xt[i % 2]
                for k in range(KC):
                    s = sq_ctr[0] % 2
                    sq_ctr[0] += 1
                    P.add("act", (lambda h, k=k, s=s: h.activation(out=sqc[s][:], in_=xt[:, k, :], func=AF.Square)),
                          R=[bx[k]], W=[b_sqc[s]])
                    P.mm(psum[:, PS_MS, :], self.ones_s[:], sqc[s][:], (k == 0), (k == KC - 1),
                         R=[b_sqc[s], self.b_ones], W=[b_ps[PS_MS]])
                P.add("act", lambda h: h.activation(out=rstd[:], in_=psum[:, PS_MS, :], func=AF.Ln, bias=self.epsc[:], scale=1.0),
                      R=[b_ps[PS_MS], self.b_ones], W=[b_rstd])
                P.add("act", lambda h: h.activation(out=rstd[:], in_=rstd[:], func=AF.Exp, scale=-0.5), R=[b_rstd], W=[b_rstd])
                for k in range(KC):
                    P.add("dve", (lambda h, k=k: h.scalar_tensor_tensor(
                        out=xn[:, k, :], in0=xt[:, k, :], scalar=self.pcol(gbase + k), in1=rstd[:],
                        op0=ALU.mult, op1=ALU.mult)), R=[bx[k], b_rstd, self.b_par], W=[b_xn[k]])

            def gateup(i):
                for j in range(FC):
                    pg, pu = PS_G[j % 2], PS_U[j % 2]
                    for k in range(KC):
                        P.mm(psum[:, pg, :], wg[:, k, j * 128:(j + 1) * 128], xn[:, k, :], (k == 0), (k == KC - 1),
                             R=b_wg[k] + [b_xn[k]], W=[b_ps[pg]])
                    for k in range(KC):
                        P.mm(psum[:, pu, :], wu[:, k, j * 128:(j + 1) * 128], xn[:, k, :], (k == 0), (k == KC - 1),
                             R=b_wu[k] + [b_xn[k]], W=[b_ps[pu]])
                    s = j % 2
                    P.add("act", (lambda h, pg=pg, s=s: h.activation(out=sgs[s][:], in_=psum[:, pg, :], func=AF.Silu)),
                          R=[b_ps[pg]], W=[b_sg[s]])
                    P.add("dve", (lambda h, j=j, pu=pu, s=s: h.tensor_tensor(
                        out=hid[:, j, :], in0=sgs[s][:], in1=psum[:, pu, :], op=ALU.mult)),
                        R=[b_sg[s], b_ps[pu]], W=[b_hid[j]])
                    group_done()

            def final_rstd(i):
                P.add("act", lambda h: h.activation(out=rstdF[:], in_=psum[:, PS_MF, :], func=AF.Ln, bias=self.epsc[:], scale=1.0),
                      R=[b_ps[PS_MF], self.b_ones], W=[b_rstdF])
                P.add("act", lambda h: h.activation(out=rstdF[:], in_=rstdF[:], func=AF.Exp, scale=-0.5),
                      R=[b_rstdF], W=[b_rstdF])

            def final_scale(i, k0, k1, last):
                xt = xts[i % 2]
                bx = b_xt[i % 2]
                for k in range(k0, k1):
                    P.add("dve", (lambda h, k=k: h.scalar_tensor_tensor(
                        out=xt[:, k, :], in0=xt[:, k, :], scalar=self.pcol(PC_GF + k), in1=rstdF[:],
                        op0=ALU.mult, op1=ALU.mult)), R=[bx[k], b_rstdF, self.b_par], W=[bx[k]])
                if last:
                    store(i)

            def store(i):
                xt = xts[i % 2]
                P.add("sp", lambda h: h.dma_start(out=self.dview(dst, dst_off + i * T, T), in_=xt[:]),
                      R=b_xt[i % 2], dma=True)
                if i + 2 < ntiles:
                    load(i + 2)

            def down(i):
                xt = xts[i % 2]
                bx = b_xt[i % 2]
                for m in range(KC):
                    py = PS_Y[m % 2]
                    for j in range(FC):
                        P.mm(psum[:, py, :], wd[:, j, m * 128:(m + 1) * 128], hid[:, j, :], (j == 0), (j == FC - 1),
                             R=b_wd[j] + [b_hid[j]], W=[b_ps[py]])
                    P.add("dve", (lambda h, m=m, py=py: h.scalar_tensor_tensor(
                        out=xt[:, m, :], in0=psum[:, py, :], scalar=0.5, in1=xt[:, m, :],
                        op0=ALU.mult, op1=ALU.add)), R=[b_ps[py], bx[m]], W=[bx[m]])
                    if final:
                        s = sqf_ctr[0] % 2
                        sqf_ctr[0] += 1
                        P.add("act", (lambda h, m=m, s=s: h.activation(out=sqf[s][:], in_=xt[:, m, :], func=AF.Square)),
                              R=[bx[m]], W=[b_sqf[s]])

                        def msf(m=m, s=s):
                            P.mm(psum[:, PS_MF, :], self.ones_s[:], sqf[s][:], (m == 0), (m == KC - 1),
                                 R=[b_sqf[s], self.b_ones], W=[b_ps[PS_MF]])
                        hooks.append(msf)
                    group_done()
                if final:
                    hooks.append(lambda: final_rstd(i))
                    for q in range(4):
                        hooks.append(lambda q=q: final_scale(i, 2 * q, 2 * q + 2, q == 3))
                else:
                    store(i)

            load(0)
            if ntiles > 1:
                load(1)
            norm(0)
            for i in range(ntiles):
                gateup(i)
                if i + 1 < ntiles:
                    hooks.append(lambda i=i: norm(i + 1))
                down(i)
            while hooks:
                hooks.popleft()()
            P.barrier()

    def phase2(self):
        nc, P = self.nc, self.P
        ntiles = self.n512
        WH = T + 3
        WA = T + 4
        with ExitStack() as es:
            wxy = es.enter_context(self._sbt("wxy", [128, KC, 2048], BF16))
            xts = [es.enter_context(self._sbt("xt%d" % i, [128, KC, WA], F32)) for i in range(2)]
            hts = [es.enter_context(self._sbt("ht%d" % i, [128, KC, WA], BF16)) for i in range(2)]
            rstds = [es.enter_context(self._sbt("rstd%d" % i, [128, WA], F32)) for i in range(2)]
            xro = [es.enter_context(self._sbt("xro%d" % i, [128, KC, WA], F32)) for i in range(2)]
            xco = [es.enter_context(self._sbt("xco%d" % i, [128, KC, T], F32)) for i in range(2)]
            gyo = [es.enter_context(self._sbt("gyo%d" % i, [128, KC, T], F32)) for i in range(2)]
            xcb = [es.enter_context(self._sbt("xcb%d" % i, [128, KC, T], BF16)) for i in range(2)]
            b_xcb = [Buf("xcb0"), Buf("xcb1")]
            psum = es.enter_context(self._pst("ps2", [128, 8, T], F32))
            b_xt = [Buf("xt0"), Buf("xt1")]
            b_ht = [Buf("ht0"), Buf("ht1")]
            b_rstd = [Buf("r0"), Buf("r1")]
            b_xro = [[Buf("xro") for _ in range(KC)] for _ in range(2)]
            b_xco = [[Buf("xco") for _ in range(KC)] for _ in range(2)]
            b_gyo = [Buf("gyo0"), Buf("gyo1")]
            b_ps = [Buf("ps%d" % i) for i in range(8)]
            b_w = self.load_w(P, wxy, "wxy", self.w["w_in"], KC, 2048, c0=1536)

            def load(i):
                P.add("sp", lambda h: h.dma_start(out=xts[i % 2][:, :, 0:WH], in_=self.dview(self.x1_s, PAD + i * T - 2, WH)),
                      W=[b_xt[i % 2]], dma=True)

            def norm(i):
                s = i % 2
                self.norm_full(P, xts[s], b_xt[s], WH, PC_GM, hts[s], b_ht[s],
                               [(psum[:, 6, :], b_ps[6]), (psum[:, 7, :], b_ps[7])],
                               rstds[s], b_rstd[s], hts[s], b_ht[s])

            pctr = [0]

            def proj(i):
                s = i % 2
                ht = hts[s]
                fL = PC_FLAG + i
                fR = PC_FLAG + self.n512 + i
                bxr = b_xro[s][0]
                for m in range(8):
                    for (c0, cw) in pieces(WH):
                        pb = pctr[0] % 6
                        pctr[0] += 1
                        for k in range(KC):
                            P.mm(psum[:, pb, 0:cw], wxy[:, k, m * 128:(m + 1) * 128], ht[:, k, c0:c0 + cw],
                                 (k == 0), (k == KC - 1), R=b_w[k] + [b_ht[s]], W=[b_ps[pb]])
                        P.add("act", (lambda h, m=m, pb=pb, c0=c0, cw=cw: h.activation(
                            out=xro[s][:, m, c0:c0 + cw], in_=psum[:, pb, 0:cw], func=AF.Copy)),
                            R=[b_ps[pb]], W=[bxr])
                P.add("dve", (lambda h: h.tensor_scalar(
                    out=xro[s][:, :, 0:2], in0=xro[s][:, :, 0:2], scalar1=self.pcol(fL), scalar2=None, op0=ALU.mult)),
                    R=[bxr, self.b_par], W=[bxr])
                P.add("dve", (lambda h: h.tensor_scalar(
                    out=xro[s][:, :, T + 2:T + 3], in0=xro[s][:, :, T + 2:T + 3], scalar1=self.pcol(fR), scalar2=None,
                    op0=ALU.mult)), R=[bxr, self.b_par], W=[bxr])

                def conv(m):
                    P.add("dve", (lambda h: h.tensor_scalar(
                        out=xco[s][:, m, :], in0=xro[s][:, m, 0:T], scalar1=self.pcol(PC_CW + m), scalar2=self.pcol(PC_CB + m),
                        op0=ALU.mult, op1=ALU.add)), R=[bxr, self.b_par], W=[b_xco[s][m]])
                    for tap in range(1, 4):
                        P.add("dve", (lambda h, tap=tap: h.scalar_tensor_tensor(
                            out=xco[s][:, m, :], in0=xro[s][:, m, tap:tap + T], scalar=self.pcol(PC_CW + tap * 8 + m),
                            in1=xco[s][:, m, :], op0=ALU.mult, op1=ALU.add)),
                            R=[bxr, self.b_par, b_xco[s][m]], W=[b_xco[s][m]])
                    P.add("pool", (lambda h: h.tensor_copy(out=xcb[s][:, m, :], in_=xco[s][:, m, :])),
                          R=[b_xco[s][m]], W=[b_xcb[s]])

                for m in range(8):
                    pb = pctr[0] % 6
                    pctr[0] += 1
                    for k in range(KC):
                        P.mm(psum[:, pb, :], wxy[:, k, 1024 + m * 128:1024 + (m + 1) * 128], ht[:, k, 2:2 + T],
                             (k == 0), (k == KC - 1), R=b_w[k] + [b_ht[s]], W=[b_ps[pb]])
                    P.add("act", (lambda h, m=m, pb=pb: h.activation(out=gyo[s][:, m, :], in_=psum[:, pb, :],
                                                                     func=AF.Gelu_apprx_tanh)),
                          R=[b_ps[pb]], W=[b_gyo[s]])
                    conv(m)
                    if m == 3 and i + 1 < ntiles:
                        norm(i + 1)
                P.add("sp", lambda h: h.dma_start(out=self.dview(self.xr_s, PAD + i * T, T), in_=xco[s][:]),
                      R=b_xco[s], dma=True)
                P.add("sp", lambda h: h.dma_start(out=self.dview(self.gy_s, i * T, T), in_=gyo[s][:]),
                      R=[b_gyo[s]], dma=True)
                P.add("sp", lambda h: h.dma_start(out=self.dview(self.xcb_s, i * T, T), in_=xcb[s][:]),
                      R=[b_xcb[s]], dma=True)
                if i + 2 < ntiles:
                    load(i + 2)

            load(0)
            if ntiles > 1:
                load(1)
            norm(0)
            for i in range(ntiles):
                proj(i)
            P.barrier()

    def rnn_phase(self, fwd):
        nc, P = self.nc, self.P
        RT = 2048
        ntl = self.NT // RT
        tiles = list(range(ntl)) if fwd else list(range(ntl - 1, -1, -1))
        wa_d = self.w["lru_wa_f" if fwd else "lru_wa_b"]
        wi_d = self.w["lru_wi_f" if fwd else "lru_wi_b"]
        PBA, PBI, PLAM = (PC_BAF, PC_BIF, PC_LAMF) if fwd else (PC_BAB, PC_BIB, PC_LAMB)
        with ExitStack() as es:
            wa = es.enter_context(self._sbt("wa", [128, KC, 128], BF16))
            wi = es.enter_context(self._sbt("wi", [128, KC, 128], BF16))
            cc = es.enter_context(self._sbt("cc", [128, 3 * KC], F32))
            carry = es.enter_context(self._sbt("carry", [128, KC], F32))
            NB = 2
            xc = [es.enter_context(self._sbt("xc%d" % i, [128, RT], F32)) for i in range(NB)]
            xcb = [es.enter_context(self._sbt("xcb%d" % i, [128, RT], BF16)) for i in range(NB)]
            b_xcb = [Buf("xcb") for _ in range(NB)]
            rr = [es.enter_context(self._sbt("rr%d" % i, [128, RT], F32)) for i in range(NB)]
            ii = [es.enter_context(self._sbt("ii%d" % i, [128, RT], F32)) for i in range(NB)]
            aa = [es.enter_context(self._sbt("aa%d" % i, [128, RT], F32)) for i in range(NB)]
            a2 = [es.enter_context(self._sbt("a2%d" % i, [128, RT], F32)) for i in range(NB)]
            hh = [es.enter_context(self._sbt("hh%d" % i, [128, RT], F32)) for i in range(NB)]
            if fwd:
                hbt = [es.enter_context(self._sbt("hbt%d" % i, [128, RT], F32)) for i in range(NB)]
                gyt = [es.enter_context(self._sbt("gyt%d" % i, [128, RT], F32)) for i in range(NB)]
                hyt = [es.enter_context(self._sbt("hyt%d" % i, [128, RT], BF16)) for i in range(NB)]
                b_hbt = [Buf("hbt") for _ in range(NB)]
                b_gyt = [Buf("gyt") for _ in range(NB)]
                b_hyt = [Buf("hyt") for _ in range(NB)]
            psum = es.enter_context(self._pst("ps3", [128, 8, T], F32))
            b_wa, b_wi = Buf("wa", ro=True), Buf("wi", ro=True)
            b_cc, b_carry = Buf("cc"), Buf("carry")
            b_xc = [Buf("xc") for _ in range(NB)]
            b_rr = [Buf("rr") for _ in range(NB)]
            b_ii = [Buf("ii") for _ in range(NB)]
            b_aa = [Buf("aa") for _ in range(NB)]
            b_a2 = [Buf("a2") for _ in range(NB)]
            b_hh = [Buf("hh") for _ in range(NB)]
            b_ps = [Buf("ps%d" % i) for i in range(8)]
            P.add("pool", lambda h: h.dma_start(out=wa[:], in_=wa_d.rearrange("(k p) d -> p k d", p=128)),
                  W=[b_wa], dma=True)
            P.add("pool", lambda h: h.dma_start(out=wi[:], in_=wi_d.rearrange("(k p) d -> p k d", p=128)),
                  W=[b_wi], dma=True)
            P.add("act", lambda h: h.activation(out=cc[:, 16:24], in_=self.par[:, PLAM:PLAM + 8], func=AF.Exp, scale=-1.0),
                  R=[self.b_par], W=[b_cc])
            P.add("act", lambda h: h.activation(out=cc[:, 16:24], in_=cc[:, 16:24], func=AF.Ln, bias=1.0, scale=1.0),
                  R=[b_cc, self.b_ones], W=[b_cc])
            P.add("dve", lambda h: h.tensor_scalar(out=cc[:, 0:8], in0=cc[:, 16:24], scalar1=-8.0, scalar2=None, op0=ALU.mult),
                  R=[b_cc], W=[b_cc])
            P.add("dve", lambda h: h.tensor_scalar(out=cc[:, 8:16], in0=cc[:, 16:24], scalar1=-16.0, scalar2=None, op0=ALU.mult),
                  R=[b_cc], W=[b_cc])
            P.add("dve", lambda h: h.memset(carry[:], 0.0), W=[b_carry])

            items = [(ti, n) for ti in tiles for n in range(KC)]

            def loads(idx):
                ti, n = items[idx]
                s = idx % NB
                t0 = ti * RT
                rows = slice(n * 128, (n + 1) * 128)
                P.add("sp", (lambda h: h.dma_start(out=xc[s][:], in_=self.xr_s[rows, PAD + t0:PAD + t0 + RT])),
                      W=[b_xc[s]], dma=True)
                P.add("sp", (lambda h: h.dma_start(out=xcb[s][:], in_=self.xcb_s[rows, t0:t0 + RT])),
                      W=[b_xcb[s]], dma=True)
                if fwd:
                    P.add("sp", (lambda h: h.dma_start(out=hbt[s][:], in_=self.hb_s[rows, t0:t0 + RT])),
                          W=[b_hbt[s]], dma=True)
                    P.add("sp", (lambda h: h.dma_start(out=gyt[s][:], in_=self.gy_s[rows, t0:t0 + RT])),
                          W=[b_gyt[s]], dma=True)

            loads(0)
            pc = [0]

            def compute(idx):
                ti, n = items[idx]
                t0 = ti * RT
                fL = PC_FLAG + (t0 // T)
                fR = PC_FLAG + self.n512 + (t0 + RT) // T - 1
                s = idx % NB
                rows = slice(n * 128, (n + 1) * 128)
                if idx + 1 < len(items):
                    loads(idx + 1)
                for (c0, cw) in pieces(RT):
                    pa = pc[0] % 8
                    pb = (pc[0] + 1) % 8
                    pc[0] += 2
                    P.mm(psum[:, pa, :], wa[:, n, :], xcb[s][:, c0:c0 + T], True, True, R=[b_wa, b_xcb[s]], W=[b_ps[pa]])
                    P.mm(psum[:, pb, :], wi[:, n, :], xcb[s][:, c0:c0 + T], True, True, R=[b_wi, b_xcb[s]], W=[b_ps[pb]])
                    P.add("act", (lambda h, c0=c0, pa=pa: h.activation(
                        out=rr[s][:, c0:c0 + T], in_=psum[:, pa, :], func=AF.Sigmoid, bias=self.pcol(PBA + n), scale=1.0)),
                        R=[b_ps[pa], self.b_par], W=[b_rr[s]])
                    P.add("act", (lambda h, c0=c0, pb=pb: h.activation(
                        out=ii[s][:, c0:c0 + T], in_=psum[:, pb, :], func=AF.Sigmoid, bias=self.pcol(PBI + n), scale=1.0)),
                        R=[b_ps[pb], self.b_par], W=[b_ii[s]])
                P.add("act", (lambda h: h.activation(out=aa[s][:], in_=rr[s][:], func=AF.Exp, scale=cc[:, n:n + 1])),
                      R=[b_rr[s], b_cc], W=[b_aa[s]])
                P.add("act", (lambda h: h.activation(out=a2[s][:], in_=rr[s][:], func=AF.Exp, scale=cc[:, 8 + n:9 + n])),
                      R=[b_rr[s], b_cc], W=[b_a2[s]])
                P.add("act", (lambda h: h.activation(out=a2[s][:], in_=a2[s][:], func=AF.Sqrt, bias=1.0, scale=-1.0)),
                      R=[b_a2[s], self.b_ones], W=[b_a2[s]])
                P.add("dve", (lambda h: h.tensor_tensor(out=ii[s][:], in0=ii[s][:], in1=xc[s][:], op=ALU.mult)),
                      R=[b_ii[s], b_xc[s]], W=[b_ii[s]])
                P.add("pool", (lambda h: h.tensor_tensor(out=ii[s][:], in0=ii[s][:], in1=a2[s][:], op=ALU.mult)),
                      R=[b_ii[s], b_a2[s]], W=[b_ii[s]])
                if self.debug and idx == 0 and not fwd:
                    for di, (tt, bb) in enumerate(((xc[s], b_xc[s]), (rr[s], b_rr[s]), (aa[s], b_aa[s]), (a2[s], b_a2[s]), (ii[s], b_ii[s]))):
                        P.add("sp", (lambda h, di=di, tt=tt: h.dma_start(out=self.dbg[di], in_=tt[:])), R=[bb], dma=True)
                if fwd:
                    P.add("dve", (lambda h: h.tensor_tensor_scan(
                        out=hh[s][:], data0=aa[s][:], data1=ii[s][:], initial=carry[:, n:n + 1],
                        op0=ALU.mult, op1=ALU.add)), R=[b_aa[s], b_ii[s], b_carry], W=[b_hh[s]])
                    P.add("dve", (lambda h: h.tensor_scalar(
                        out=carry[:, n:n + 1], in0=hh[s][:, RT - 1:RT], scalar1=self.pcol(fR), scalar2=None, op0=ALU.mult)),
                        R=[b_hh[s], self.b_par], W=[b_carry])
                    P.add("dve", (lambda h: h.tensor_tensor(out=hh[s][:], in0=hh[s][:], in1=hbt[s][:], op=ALU.add)),
                          R=[b_hh[s], b_hbt[s]], W=[b_hh[s]])
                    P.add("pool", (lambda h: h.tensor_tensor(out=hyt[s][:], in0=hh[s][:], in1=gyt[s][:], op=ALU.mult)),
                          R=[b_hh[s], b_gyt[s]], W=[b_hyt[s]])
                    P.add("sp", (lambda h: h.dma_start(out=self.hy_s[rows, t0:t0 + RT], in_=hyt[s][:])),
                          R=[b_hyt[s]], dma=True)
                else:
                    P.add("dve", (lambda h: h.tensor_tensor_scan(
                        out=hh[s][:, ::-1], data0=aa[s][:, ::-1], data1=ii[s][:, ::-1], initial=carry[:, n:n + 1],
                        op0=ALU.mult, op1=ALU.add)), R=[b_aa[s], b_ii[s], b_carry], W=[b_hh[s]])
                    P.add("dve", (lambda h: h.tensor_scalar(
                        out=carry[:, n:n + 1], in0=hh[s][:, 0:1], scalar1=self.pcol(fL), scalar2=None, op0=ALU.mult)),
                        R=[b_hh[s], self.b_par], W=[b_carry])
                    P.add("sp", (lambda h: h.dma_start(out=self.hb_s[rows, t0:t0 + RT], in_=hh[s][:])),
                          R=[b_hh[s]], dma=True)

            for idx in range(len(items)):
                compute(idx)
            P.barrier()

    def phase5a(self):
        nc, P = self.nc, self.P
        ntiles = self.n512
        W = T + 2 * PAD
        with ExitStack() as es:
            wq = es.enter_context(self._sbt("wq", [128, KC, 1024], BF16))
            wkv = es.enter_context(self._sbt("wkv", [128, KC, 512], BF16))
            cst = es.enter_context(self._sbt("cst", [128, 512], F32))
            ident = es.enter_context(self._sbt("ident", [128, 128], BF16))
            bias = es.enter_context(self._sbt("bias", [128, 8, 384], F32))
            nsink = es.enter_context(self._sbt("nsink", [128, 8], F32))
            onesr = es.enter_context(self._sbt("onesr", [1, 128], BF16))
            penr = [es.enter_context(self._sbt("penr%d" % i, [1, 2, 128], BF16)) for i in range(2)]
            penf = [es.enter_context(self._sbt("penf%d" % i, [1, 2], F32)) for i in range(2)]
            xh = [es.enter_context(self._sbt("xh%d" % i, [128, KC, W], F32)) for i in range(2)]
            ht = [es.enter_context(self._sbt("ht%d" % i, [128, KC, W], BF16)) for i in range(2)]
            rstd = [es.enter_context(self._sbt("rstd%d" % i, [128, W], F32)) for i in range(2)]
            qT = [es.enter_context(self._sbt("qT%d" % i, [128, 8, T], BF16)) for i in range(2)]
            kT = [es.enter_context(self._sbt("kT%d" % i, [128, 2, W], BF16)) for i in range(2)]
            vv = [es.enter_context(self._sbt("vv%d" % i, [128, 6, 256], BF16)) for i in range(2)]
            ss = [es.enter_context(self._sbt("ss%d" % i, [128, 4, 384], F32)) for i in range(2)]
            pp = [es.enter_context(self._sbt("pp%d" % i, [128, 4, 384], F32)) for i in range(2)]
            pn = [es.enter_context(self._sbt("pn%d" % i, [128, 4, 384], BF16)) for i in range(2)]
            pT = [es.enter_context(self._sbt("pT%d" % i, [128, 4, 384], BF16)) for i in range(2)]
            st = [es.enter_context(self._sbt("st%d" % i, [128, 32], F32)) for i in range(2)]
            oT = [es.enter_context(self._sbt("oT%d" % i, [128, 8, T], BF16)) for i in range(2)]
            ps_s = es.enter_context(self._pst("ps_s", [128, 4, T], F32))
            ps_t = es.enter_context(self._pst("ps_t", [128, 2, 1024], BF16))
            ps_o = es.enter_context(self._pst("ps_o", [128, 4, 128], F32))
            ps_m = es.enter_context(self._pst("ps_m", [128, T], F32))
            b_cst = Buf("cst", ro=True)
            b_ident, b_bias, b_nsink, b_onesr = Buf("ident", ro=True), Buf("bias", ro=True), Buf("nsink", ro=True), Buf("onesr", ro=True)
            b_penr, b_penf = [Buf("penr0"), Buf("penr1")], [Buf("penf0"), Buf("penf1")]
            b_xh = [Buf("xh0"), Buf("xh1")]
            b_ht, b_rstd = [Buf("ht0"), Buf("ht1")], [Buf("rstd0"), Buf("rstd1")]
            b_qT, b_kT, b_vv = [Buf("qT0"), Buf("qT1")], [Buf("kT0"), Buf("kT1")], [Buf("vv0"), Buf("vv1")]
            b_ss = [Buf("ss0"), Buf("ss1")]
            b_pp = [Buf("pp0"), Buf("pp1")]
            b_pn = [[Buf("pn0a"), Buf("pn0b")], [Buf("pn1a"), Buf("pn1b")]]
            b_pT = [[Buf("pT0a"), Buf("pT0b")], [Buf("pT1a"), Buf("pT1b")]]
            b_stA = [Buf("stA0"), Buf("stA1")]
            b_stB = [Buf("stB0"), Buf("stB1")]
            b_stC = [Buf("stC0"), Buf("stC1")]
            b_stD = [Buf("stD0"), Buf("stD1")]
            b_oT = [Buf("oT0"), Buf("oT1")]
            b_pss = [Buf("pss%d" % i) for i in range(4)]
            b_pst = [Buf("pst0"), Buf("pst1")]
            b_pso, b_psm = Buf("pso"), Buf("psm")

            b_wq = self.load_w(P, wq, "wq", self.w["w_in"], KC, 1024, c0=0)
            b_wkv = self.load_w(P, wkv, "wkv", self.w["w_in"], KC, 512, c0=1024)
            P.add("sp", lambda h: h.dma_start(out=cst[:], in_=self.cst_d[:]), W=[b_cst], dma=True)
            P.add("dve", lambda h: h.tensor_copy(out=ident[:], in_=cst[:, 0:128]), R=[b_cst], W=[b_ident])
            for hd in range(8):
                P.add("dve", (lambda h, hd=hd: h.tensor_scalar(out=bias[:, hd, :], in0=cst[:, 128:512],
                                                               scalar1=self.pcol(PC_SLOPE + hd), scalar2=None, op0=ALU.mult)),
                      R=[b_cst, self.b_par], W=[b_bias])
            P.add("dve", lambda h: h.tensor_scalar(out=nsink[:], in0=self.par[:, PC_SINK:PC_SINK + 8], scalar1=-1.0,
                                                   scalar2=None, op0=ALU.mult), R=[self.b_par], W=[b_nsink])
            P.add("dve", lambda h: h.memset(onesr[:], 1.0), W=[b_onesr])

            def load(i):
                P.add("sp", lambda h: h.dma_start(out=xh[i % 2][:], in_=self.dview(self.x1_s, i * T, W)),
                      W=[b_xh[i % 2]], dma=True)

            normed = set()

            def norm_parts(i):
                p = i % 2
                x, bx = xh[p], b_xh[p]
                h_, bh = ht[p], b_ht[p]
                r_, br = rstd[p], b_rstd[p]
                parts = []

                def sq(k0, k1):
                    for k in range(k0, k1):
                        P.add("act", (lambda h, k=k: h.activation(out=h_[:, k, 0:W], in_=x[:, k, 0:W], func=AF.Square)),
                              R=[bx], W=[bh], acc=(k > 0))
                parts.append(lambda: sq(0, 4))
                parts.append(lambda: sq(4, 8))

                def ms(c0, cw):
                    for k in range(KC):
                        P.mm(ps_m[:, 0:cw], self.ones_s[:], h_[:, k, c0:c0 + cw], (k == 0), (k == KC - 1),
                             R=[bh, self.b_ones], W=[b_psm])
                    P.add("act", (lambda h: h.activation(out=r_[:, c0:c0 + cw], in_=ps_m[:, 0:cw], func=AF.Ln,
                                                         bias=self.epsc[:], scale=1.0)), R=[b_psm, self.b_ones], W=[br])
                    P.add("act", (lambda h: h.activation(out=r_[:, c0:c0 + cw], in_=r_[:, c0:c0 + cw], func=AF.Exp,
                                                         scale=-0.5)), R=[br], W=[br])
                for (c0, cw) in pieces(W):
                    parts.append(lambda c0=c0, cw=cw: ms(c0, cw))

                def hop(k0, k1):
                    for k in range(k0, k1):
                        P.add("dve", (lambda h, k=k: h.scalar_tensor_tensor(
                            out=h_[:, k, 0:W], in0=x[:, k, 0:W], scalar=self.pcol(PC_GM + k), in1=r_[:, 0:W],
                            op0=ALU.mult, op1=ALU.mult)), R=[bx, br, self.b_par, bh], W=[bh], acc=(k > 0))
                for j in range(4):
                    parts.append(lambda j=j: hop(2 * j, 2 * j + 2))
                return parts

            def proj_groups(i):
                p = i % 2
                x, bx = xh[p], b_xh[p]
                h_, bh = ht[p], b_ht[p]
                G = deque()
                rot = [0]

                def bank():
                    r = rot[0] % 4
                    rot[0] += 1
                    return ps_s[:, r, :], b_pss[r]

                def g_norm():
                    fl = PC_FLAG + i
                    P.add("dve", lambda h: h.tensor_scalar(out=penf[p][:, 0:1], in0=self.par[0:1, fl:fl + 1], scalar1=1e30,
                                                           scalar2=-1e30, op0=ALU.mult, op1=ALU.add),
                          R=[self.b_par], W=[b_penf[p]])
                    P.add("dve", lambda h: h.tensor_scalar(out=penf[p][:, 1:2],
                                                           in0=self.par[0:1, fl + self.n512:fl + self.n512 + 1],
                                                           scalar1=1e30, scalar2=-1e30, op0=ALU.mult, op1=ALU.add),
                          R=[self.b_par], W=[b_penf[p]])
                    for sd in range(2):
                        P.add("dve", (lambda h, sd=sd: h.tensor_scalar(out=penr[p][:, sd, :], in0=onesr[:],
                                                                       scalar1=penf[p][:, sd:sd + 1], scalar2=None, op0=ALU.mult)),
                              R=[b_penf[p], b_onesr], W=[b_penr[p]])
                    if i not in normed:
                        self.norm_full(P, x, bx, W, PC_GM, h_, bh, [(ps_m, b_psm)], rstd[p], b_rstd[p], h_, bh)
                G.append(g_norm)

                def g_q(m):
                    pm, bpm = bank()
                    for k in range(KC):
                        P.mm(pm[:, :], wq[:, k, m * 128:(m + 1) * 128], h_[:, k, PAD:PAD + T],
                             (k == 0), (k == KC - 1), R=b_wq[k] + [bh], W=[bpm])
                    if m % 2 == 0:
                        P.add("act", (lambda h: h.activation(out=qT[p][:, m, :], in_=pm[:, :], func=AF.Copy, scale=SCALE)),
                              R=[bpm], W=[b_qT[p]])
                    else:
                        P.add("dve", (lambda h: h.tensor_scalar(out=qT[p][:, m, :], in0=pm[:, :], scalar1=SCALE,
                                                                scalar2=None, op0=ALU.mult)), R=[bpm], W=[b_qT[p]])
                for m in range(8):
                    G.append(lambda m=m: g_q(m))

                def g_k(g, c0, cw):
                    pm, bpm = bank()
                    for k in range(KC):
                        P.mm(pm[:, 0:cw], wkv[:, k, g * 128:(g + 1) * 128], h_[:, k, c0:c0 + cw],
                             (k == 0), (k == KC - 1), R=b_wkv[k] + [bh], W=[bpm])
                    P.add("dve", (lambda h: h.tensor_copy(out=kT[p][:, g, c0:c0 + cw], in_=pm[:, 0:cw])),
                          R=[bpm], W=[b_kT[p]])
                for g in range(2):
                    for (c0, cw) in pieces(W):
                        G.append(lambda g=g, c0=c0, cw=cw: g_k(g, c0, cw))

                def g_v(b):
                    pm, bpm = bank()
                    for k in range(KC):
                        P.mm(pm[:, 0:256], h_[:, k, b * 128:(b + 1) * 128], wkv[:, k, 256:512],
                             (k == 0), (k == KC - 1), R=b_wkv[k] + [bh], W=[bpm])
                    P.add("act", (lambda h: h.activation(out=vv[p][:, b, :], in_=pm[:, 0:256], func=AF.Copy)),
                          R=[bpm], W=[b_vv[p]])
                for b in range(6):
                    G.append(lambda b=b: g_v(b))
                return G

            def tile(i, nxt):
                p = i % 2
                o = oT[p]
                bo = b_oT[p]
                qT_, kT_, vv_, penr_ = qT[p], kT[p], vv[p], penr[p]
                bq, bk, bv, bpen = b_qT[p], b_kT[p], b_vv[p], b_penr[p]

                def slot():
                    if nxt:
                        nxt.popleft()()

                blocks = [(qb, g) for qb in range(4) for g in range(2)]

                def A(bi):
                    qb, g = blocks[bi]
                    has_pen = (qb == 0) or (qb == 3)
                    for hh in range(4):
                        hd = g * 4 + hh
                        P.mm(ps_s[:, hh, 0:384], qT_[:, hd, qb * 128:(qb + 1) * 128], kT_[:, g, qb * 128:qb * 128 + 384],
                             True, (not has_pen), R=[bq, bk], W=[b_pss[hh]])
                        if has_pen:
                            sd = 0 if qb == 0 else 1
                            cc0 = 0 if qb == 0 else 256
                            P.mm(ps_s[:, hh, cc0:cc0 + 128], onesr[:], penr_[:, sd, :], False, True,
                                 R=[b_onesr, bpen], W=[b_pss[hh]])

                def B1(bi):
                    qb, g = blocks[bi]
                    s = bi % 2
                    P.add("dve", (lambda h: h.tensor_tensor(
                        out=ss[s][:], in0=ps_s[:, :, 0:384], in1=bias[:, g * 4:(g + 1) * 4, :], op=ALU.add)),
                        R=b_pss + [b_bias], W=[b_ss[s]])

                def B2(bi):
                    qb, g = blocks[bi]
                    s = bi % 2
                    P.add("dve", (lambda h: h.tensor_reduce(out=st[s][:, 0:4], in_=ss[s][:], axis=AX.X, op=ALU.max)),
                          R=[b_ss[s]], W=[b_stA[s]])
                    P.add("dve", (lambda h: h.scalar_tensor_tensor(
                        out=st[s][:, 4:8], in0=st[s][:, 0:4], scalar=-1.0, in1=nsink[:, g * 4:(g + 1) * 4],
                        op0=ALU.mult, op1=ALU.min)), R=[b_stA[s], b_nsink], W=[b_stA[s]])
                    P.add("dve", (lambda h: h.tensor_tensor(
                        out=st[s][:, 24:28], in0=st[s][:, 4:8], in1=nsink[:, g * 4:(g + 1) * 4], op=ALU.subtract)),
                        R=[b_stA[s], b_nsink], W=[b_stA[s]])

                def C(bi):
                    s = bi % 2
                    P.add("act", (lambda h: h.activation(out=st[s][:, 12:16], in_=st[s][:, 24:28], func=AF.Exp)),
                          R=[b_stA[s]], W=[b_stC[s]])
                    for hh in range(4):
                        P.add("act", (lambda h, hh=hh: h.activation(
                            out=pp[s][:, hh, :], in_=ss[s][:, hh, :], func=AF.Exp, bias=st[s][:, 4 + hh:5 + hh], scale=1.0,
                            accum_out=st[s][:, 8 + hh:9 + hh])), R=[b_ss[s], b_stA[s]], W=[b_pp[s], b_stB[s]], acc=(hh > 0))

                def Dd(bi):
                    s = bi % 2
                    P.add("dve", (lambda h: h.tensor_tensor(
                        out=st[s][:, 16:20], in0=st[s][:, 8:12], in1=st[s][:, 12:16], op=ALU.add)),
                        R=[b_stB[s], b_stC[s]], W=[b_stD[s]])
                    P.add("dve", (lambda h: h.reciprocal(out=st[s][:, 20:24], in_=st[s][:, 16:20])),
                          R=[b_stD[s]], W=[b_stD[s]])

                def D2(bi):
                    s = bi % 2
                    for hh in range(4):
                        if hh < 2:
                            P.add("act", (lambda h, hh=hh: h.activation(
                                out=pn[s][:, hh, :], in_=pp[s][:, hh, :], func=AF.Copy, scale=st[s][:, 20 + hh:21 + hh])),
                                R=[b_pp[s], b_stD[s]], W=[b_pn[s][0]], acc=(hh > 0))
                        else:
                            P.add("dve", (lambda h, hh=hh: h.tensor_scalar(
                                out=pn[s][:, hh, :], in0=pp[s][:, hh, :], scalar1=st[s][:, 20 + hh:21 + hh], scalar2=None,
                                op0=ALU.mult)), R=[b_pp[s], b_stD[s]], W=[b_pn[s][1]], acc=(hh > 2))

                def E(bi):
                    s = bi % 2
                    for hh in range(4):
                        bk_ = hh // 2
                        for c in range(3):
                            off = (hh % 2) * 384 + c * 128
                            P.add("pe", (lambda h, hh=hh, c=c, bk_=bk_, off=off: h.transpose(
                                ps_t[:, bk_, off:off + 128], pn[s][:, hh, c * 128:(c + 1) * 128], ident[:])),
                                R=[b_pn[s][hh // 2], b_ident], W=[b_pst[bk_]], acc=(not (hh % 2 == 0 and c == 0)))

                def F(bi):
                    s = bi % 2
                    P.add("act", (lambda h: h.activation(out=pT[s][:, 0:2, :], in_=ps_t[:, 0, 0:768].rearrange(
                        "p (a b) -> p a b", a=2), func=AF.Copy)), R=[b_pst[0]], W=[b_pT[s][0]])
                    P.add("dve", (lambda h: h.tensor_copy(out=pT[s][:, 2:4, :], in_=ps_t[:, 1, 0:768].rearrange(
                        "p (a b) -> p a b", a=2))), R=[b_pst[1]], W=[b_pT[s][1]])

                def G(bi):
                    qb, g = blocks[bi]
                    s = bi % 2
                    for hh in range(4):
                        for c in range(3):
                            P.mm(ps_o[:, hh, :], vv_[:, qb + c, g * 128:(g + 1) * 128], pT[s][:, hh, c * 128:(c + 1) * 128],
                                 (c == 0), (c == 2), R=[bv, b_pT[s][hh // 2]], W=[b_pso], acc=(not (hh == 0 and c == 0)))

                def H(bi):
                    qb, g = blocks[bi]
                    P.add("act", (lambda h: h.activation(
                        out=o[:, g * 4:(g + 1) * 4, qb * 128:(qb + 1) * 128], in_=ps_o[:], func=AF.Copy)),
                        R=[b_pso], W=[bo])

                NBK = len(blocks)
                nparts = norm_parts(i + 1) if i + 1 < ntiles else []
                if i + 1 < ntiles:
                    normed.add(i + 1)
                if i + 2 < ntiles:
                    load(i + 2)
                A(0)
                B1(0)
                B2(0)
                if NBK > 1:
                    A(1)
                C(0)
                if NBK > 1:
                    B1(1)
                for bi in range(NBK):
                    if bi + 1 < NBK:
                        B2(bi + 1)
                    if bi >= 1:
                        F(bi - 1)
                        G(bi - 1)
                    if bi + 2 < NBK:
                        A(bi + 2)
                    if bi + 1 < NBK:
                        C(bi + 1)
                    Dd(bi)
                    D2(bi)
                    if bi + 2 < NBK:
                        B1(bi + 2)
                    if bi >= 1:
                        H(bi - 1)
                    E(bi)
                    if bi < len(nparts):
                        nparts[bi]()
                F(NBK - 1)
                G(NBK - 1)
                H(NBK - 1)
                while nxt:
                    nxt.popleft()()
                P.add("sp", lambda h: h.dma_start(out=self.dview(self.ot_s, i * T, T), in_=o[:]), R=[bo], dma=True)

            load(0)
            if ntiles > 1:
                load(1)
            for i in range(ntiles):
                g0 = proj_groups(i)
                while g0:
                    g0.popleft()()
                tile(i, deque())
            P.barrier()

    def phase5b(self):
        nc, P = self.nc, self.P
        ntiles = self.n512
        with ExitStack() as es:
            wg2 = es.enter_context(self._sbt("wg2", [128, KC, 2048], BF16))
            wao = es.enter_context(self._sbt("wao", [128, KC, D], BF16))
            wro = es.enter_context(self._sbt("wro", [128, KC, D], BF16))
            wout = es.enter_context(self._sbt("wout", [128, KC, D], BF16))
            xts = [es.enter_context(self._sbt("xt%d" % i, [128, KC, T], F32)) for i in range(2)]
            ots = [es.enter_context(self._sbt("ot%d" % i, [128, KC, T], BF16)) for i in range(2)]
            hys = [es.enter_context(self._sbt("hy%d" % i, [128, KC, T], BF16)) for i in range(2)]
            ht = es.enter_context(self._sbt("ht", [128, KC, T], BF16))
            rstd = es.enter_context(self._sbt("rstd", [128, T], F32))
            mg = es.enter_context(self._sbt("mg", [128, KC, T], BF16))
            sg = [es.enter_context(self._sbt("sg%d" % i, [128, T], F32)) for i in range(2)]
            mA = [es.enter_context(self._sbt("mA%d" % i, [128, T], F32)) for i in range(2)]
            mB = [es.enter_context(self._sbt("mB%d" % i, [128, T], F32)) for i in range(2)]
            psum = es.enter_context(self._pst("ps5", [128, 8, T], F32))
            b_xt = [Buf("xt0"), Buf("xt1")]
            b_ot = [Buf("ot0"), Buf("ot1")]
            b_hy = [Buf("hy0"), Buf("hy1")]
            b_ht, b_rstd = Buf("ht"), Buf("rstd")
            b_mg = [Buf("mg%d" % k) for k in range(KC)]
            b_sg = [Buf("sg0"), Buf("sg1")]
            b_mA, b_mB = [Buf("mA0"), Buf("mA1")], [Buf("mB0"), Buf("mB1")]
            b_ps = [Buf("ps%d" % i) for i in range(8)]
            b_wg2 = self.load_w(P, wg2, "wg2", self.w["w_in"], KC, 2048, c0=3584)
            b_wao = self.load_w(P, wao, "wao", self.w["w_attn_o"], KC, D)
            b_wro = self.load_w(P, wro, "wro", self.w["w_rnn_o"], KC, D)
            b_wout = self.load_w(P, wout, "wout", self.w["w_out"], KC, D)

            def load(i):
                s = i % 2
                P.add("sp", lambda h: h.dma_start(out=xts[s][:], in_=self.dview(self.x1_s, PAD + i * T, T)),
                      W=[b_xt[s]], dma=True)
                P.add("sp", lambda h: h.dma_start(out=ots[s][:], in_=self.dview(self.ot_s, i * T, T)), W=[b_ot[s]], dma=True)
                P.add("sp", lambda h: h.dma_start(out=hys[s][:], in_=self.dview(self.hy_s, i * T, T)), W=[b_hy[s]], dma=True)

            def norm(i, stage=None):
                s = i % 2
                self.norm_full(P, xts[s], b_xt[s], T, PC_GM, ht, b_ht, [(psum[:, 6, :], b_ps[6])],
                               rstd, b_rstd, ht, b_ht, stage=stage)

            step = [0]

            def tile(i):
                s = i % 2
                for m in range(KC):
                    u = m % 2
                    for br in range(2):
                        pr = 2 * (step[0] % 2)
                        step[0] += 1
                        if br == 0:
                            wt, bw, src, bsrc, gcol = wao, b_wao, ots[s], b_ot[s], m * 128
                        else:
                            wt, bw, src, bsrc, gcol = wro, b_wro, hys[s], b_hy[s], 1024 + m * 128
                        for k in range(KC):
                            P.mm(psum[:, pr, :], wt[:, k, m * 128:(m + 1) * 128], src[:, k, :], (k == 0), (k == KC - 1),
                                 R=bw[k] + [bsrc], W=[b_ps[pr]])
                        for k in range(KC):
                            P.mm(psum[:, pr + 1, :], wg2[:, k, gcol:gcol + 128], ht[:, k, :], (k == 0), (k == KC - 1),
                                 R=b_wg2[k] + [b_ht], W=[b_ps[pr + 1]])
                        P.add("act", (lambda h, br=br, pr=pr: h.activation(out=sg[br][:], in_=psum[:, pr + 1, :], func=AF.Sigmoid)),
                              R=[b_ps[pr + 1]], W=[b_sg[br]])
                        mo, bmo = (mA[u], b_mA[u]) if br == 0 else (mB[u], b_mB[u])
                        P.add("dve", (lambda h, br=br, pr=pr, mo=mo: h.tensor_tensor(
                            out=mo[:], in0=sg[br][:], in1=psum[:, pr, :], op=ALU.mult)),
                            R=[b_sg[br], b_ps[pr]], W=[bmo])
                    P.add("dve", (lambda h, u=u, m=m: h.tensor_tensor(out=mg[:, m, :], in0=mA[u][:], in1=mB[u][:], op=ALU.add)),
                          R=[b_mA[u], b_mB[u]], W=[b_mg[m]])
                if i + 1 < ntiles:
                    norm(i + 1, "sq")
                for m in range(KC):
                    pb = 4 + (m % 2)
                    for k in range(KC):
                        P.mm(psum[:, pb, :], wout[:, k, m * 128:(m + 1) * 128], mg[:, k, :], (k == 0), (k == KC - 1),
                             R=b_wout[k] + [b_mg[k]], W=[b_ps[pb]])
                    P.add("dve", (lambda h, pb=pb, m=m: h.tensor_tensor(
                        out=xts[s][:, m, :], in0=psum[:, pb, :], in1=xts[s][:, m, :], op=ALU.add)),
                        R=[b_ps[pb], b_xt[s]], W=[b_xt[s]])
                    if i + 1 < ntiles:
                        if m == 2:
                            norm(i + 1, "ms")
                        if m >= 4:
                            norm(i + 1, ("h", 2 * (m - 4), 2 * (m - 4) + 2))
                P.add("sp", lambda h: h.dma_start(out=self.dview(self.x2_s, i * T, T), in_=xts[s][:]),
                      R=[b_xt[s]], dma=True)
                if i + 2 < ntiles:
                    load(i + 2)

            load(0)
            if ntiles > 1:
                load(1)
            norm(0)
            for i in range(ntiles):
                tile(i)
            P.barrier()


def _pk(v):
    return np.ascontiguousarray(np.asarray(v, np.float32).reshape(8, 128).T)


def make_consts():
    cst = np.zeros((128, 512), np.float32)
    cst[:, 0:128] = np.eye(128, dtype=np.float32)
    q = np.arange(128)[:, None]
    c = np.arange(384)[None, :]
    dist = np.abs(q + 128 - c).astype(np.float32)
    cst[:, 128:512] = np.where(dist <= 128, -dist, -1e30).astype(np.float32)
    return cst


def make_par(inp, seq_starts, NT):
    n512 = NT // T
    par = np.zeros((128, PC_FLAG + 2 * n512), np.float32)
    par[:, PC_G1:PC_G1 + 8] = _pk(inp["norm_ffn1"][0])
    par[:, PC_GM:PC_GM + 8] = _pk(inp["norm_mix"][0])
    par[:, PC_G2:PC_G2 + 8] = _pk(inp["norm_ffn2"][0])
    par[:, PC_GF:PC_GF + 8] = _pk(inp["norm_final"])
    for tap in range(4):
        par[:, PC_CW + tap * 8:PC_CW + tap * 8 + 8] = _pk(inp["conv_w"][0, tap])
    par[:, PC_CB:PC_CB + 8] = _pk(inp["conv_b"][0])
    par[:, PC_BAF:PC_BAF + 8] = _pk(inp["lru_ba_f"][0])
    par[:, PC_BIF:PC_BIF + 8] = _pk(inp["lru_bi_f"][0])
    par[:, PC_LAMF:PC_LAMF + 8] = _pk(inp["lru_lam_f"][0])
    par[:, PC_BAB:PC_BAB + 8] = _pk(inp["lru_ba_b"][0])
    par[:, PC_BIB:PC_BIB + 8] = _pk(inp["lru_bi_b"][0])
    par[:, PC_LAMB:PC_LAMB + 8] = _pk(inp["lru_lam_b"][0])
    par[:, PC_SINK:PC_SINK + 8] = np.asarray(inp["attn_sink"], np.float32).reshape(1, 8)
    par[:, PC_SLOPE:PC_SLOPE + 8] = (2.0 ** (-np.arange(1, 9, dtype=np.float64))).astype(np.float32)[None, :]
    starts = set(int(s) // T for s in seq_starts)
    for i in range(n512):
        par[:, PC_FLAG + i] = 0.0 if i in starts else 1.0
        par[:, PC_FLAG + n512 + i] = 0.0 if ((i + 1) in starts or i + 1 == n512) else 1.0
    return par


def weight_map(inp):
    f = lambda a: np.ascontiguousarray(np.asarray(a, np.float32))
    return {
        "ffn1_gate": f(inp["ffn1_gate"][0]), "ffn1_up": f(inp["ffn1_up"][0]), "ffn1_down": f(inp["ffn1_down"][0]),
        "w_in": f(inp["w_in"][0]),
        "lru_wa_f": f(inp["lru_wa_f"][0]).reshape(1024, 128), "lru_wi_f": f(inp["lru_wi_f"][0]).reshape(1024, 128),
        "lru_wa_b": f(inp["lru_wa_b"][0]).reshape(1024, 128), "lru_wi_b": f(inp["lru_wi_b"][0]).reshape(1024, 128),
        "w_attn_o": f(inp["w_attn_o"][0]), "w_rnn_o": f(inp["w_rnn_o"][0]), "w_out": f(inp["w_out"][0]),
        "ffn2_gate": f(inp["ffn2_gate"][0]), "ffn2_up": f(inp["ffn2_up"][0]), "ffn2_down": f(inp["ffn2_down"][0]),
    }


_NC_CACHE = {}


def kernel(**inp):
    NT = NT_FULL
    xp = np.asarray(inp["x_prompt"], np.float32)
    xs = np.asarray(inp["x_sample"], np.float32)
    S = xs.shape[1]
    counts = [6, 6, 5, 5, 5, 5]
    assign = []
    b = 0
    for c in counts:
        assign.append(list(range(b, b + c)))
        b += c
    wm = weight_map(inp)
    cst = make_consts()
    in_maps = []
    for core in range(NCORES):
        xT = np.zeros((D, NT), np.float32)
        if core < 2:
            xT[:, :] = xp[core].T
            starts = [0]
        else:
            ids = assign[core - 2]
            for j, sid in enumerate(ids):
                xT[:, j * S:(j + 1) * S] = xs[sid].T
            starts = [j * S for j in range(NT // S)]
        m = {"xT": xT, "par": make_par(inp, starts, NT), "cst": cst}
        m.update(wm)
        in_maps.append(m)
    if NT not in _NC_CACHE:
        _NC_CACHE[NT] = Builder(NT).build()
    nc = _NC_CACHE[NT]
    res = run_bass_kernel_spmd(nc, in_maps, core_ids=list(range(NCORES)))
    y_prompt = np.empty_like(xp)
    y_sample = np.empty_like(xs)
    for core in range(NCORES):
        yT = res.results[core]["yT"]
        if core < 2:
            y_prompt[core] = yT.T
        else:
            ids = assign[core - 2]
            for j, sid in enumerate(ids):
                y_sample[sid] = yT[:, j * S:(j + 1) * S].T
    return (y_prompt, y_sample)
```

```python
import numpy as np
from collections import deque
from contextlib import ExitStack

import concourse.bass as bass
import concourse.mybir as mybir
from concourse.bass_utils import run_bass_kernel_spmd

F32 = mybir.dt.float32
BF16 = mybir.dt.bfloat16
AF = mybir.ActivationFunctionType
ALU = mybir.AluOpType
AX = mybir.AxisListType

D = 1024
KC = 8
DFF = 2816
FC = 22
T = 512
EPS = 1e-6
NCORES = 8
NT_FULL = 16384
PAD = 128
SCALE = 128 ** -0.5

PC_G1, PC_GM, PC_G2, PC_GF = 0, 8, 16, 24
PC_CW = 32
PC_CB = 64
PC_BAF, PC_BIF, PC_LAMF = 72, 80, 88
PC_BAB, PC_BIB, PC_LAMB = 96, 104, 112
PC_SINK = 120
PC_SLOPE = 128
PC_FLAG = 136


class Buf:
    __slots__ = ("name", "w", "rs", "ro")

    def __init__(self, name, ro=False):
        self.name = name
        self.w = None
        self.rs = {}
        self.ro = ro


class Op:
    __slots__ = ("eng", "fn", "deps", "is_dma", "signal", "sigval", "dsem", "dval", "seq", "small")


ENGS = ("pe", "act", "dve", "pool", "sp")
NRING = 8


class Prog:
    def __init__(self, nc, es):
        self.nc = nc
        self.h = {"pe": nc.tensor, "act": nc.scalar, "dve": nc.vector, "pool": nc.gpsimd, "sp": nc.sync}
        self.sems = []
        self.esem = {}
        for e in ENGS:
            self.esem[e] = len(self.sems)
            self.sems.append(es.enter_context(nc.semaphore("s_" + e)))
        self.ring = {}
        for q in ("sp", "pool", "act"):
            self.ring[q] = []
            for i in range(NRING):
                self.ring[q].append(len(self.sems))
                self.sems.append(es.enter_context(nc.semaphore("d_%s%d" % (q, i))))
        self.cnt = {e: 0 for e in ENGS}
        self.ringcnt = {q: 0 for q in self.ring}
        self.waited = {e: {} for e in ENGS}
        self.ops = {e: [] for e in ENGS}
        self.last = {e: None for e in ENGS}
        self.dmas = []
        self.seq = 0
        self.nins = 0

    def add(self, eng, fn, R=(), W=(), dma=False, deps=(), small=False, acc=False):
        op = Op()
        op.small = small
        wprev = set(id(b.w) for b in W if b.w is not None) if acc else ()
        op.eng = eng
        op.fn = fn
        op.is_dma = dma
        op.signal = False
        op.sigval = 0
        op.dsem = -1
        op.dval = 0
        op.seq = self.seq
        self.seq += 1
        dl = {}
        for d in deps:
            dl[d.seq] = d
        for b in R:
            if b.w is not None:
                dl[b.w.seq] = b.w
        for b in W:
            if b.w is not None:
                dl[b.w.seq] = b.w
            for r in b.rs.values():
                dl[r.seq] = r
        for b in W:
            b.w = op
            b.rs = {}
        if dma:
            for b in R:
                if not b.ro:
                    b.rs[("dma", op.seq)] = op
        else:
            for b in R:
                if not b.ro:
                    b.rs[eng] = op
        dd = []
        for d in dl.values():
            if d is op:
                continue
            if acc and (not d.is_dma) and d.eng == eng and id(d) in wprev:
                continue
            dd.append(d)
            if not d.is_dma:
                d.signal = True
        op.deps = dd
        self.ops[eng].append(op)
        if dma:
            self.dmas.append(op)
        else:
            self.last[eng] = op
        return op

    def mm(self, out, lhsT, rhs, start, stop, R, W, acc=None):
        return self.add("pe", lambda h: h.matmul(out, lhsT=lhsT, rhs=rhs, start=start, stop=stop),
                        R=R, W=W, acc=((not start) if acc is None else acc))

    def flush(self):
        for e in ENGS:
            for op in self.ops[e]:
                if op.is_dma:
                    i = self.ringcnt[e]
                    self.ringcnt[e] += 1
                    op.dsem = self.ring[e][i % NRING]
                    op.dval = 16 * (i // NRING + 1)
                elif op.signal:
                    self.cnt[e] += 1
                    op.sigval = self.cnt[e]
        for e in ENGS:
            h = self.h[e]
            waited = self.waited[e]
            for op in self.ops[e]:
                need = {}
                for d in op.deps:
                    if d.is_dma:
                        s, v = d.dsem, d.dval
                    else:
                        s, v = self.esem[d.eng], d.sigval
                    if need.get(s, 0) < v:
                        need[s] = v
                if op.is_dma and op.dval > 16:
                    if need.get(op.dsem, 0) < op.dval - 16:
                        need[op.dsem] = op.dval - 16
                for s, v in need.items():
                    if waited.get(s, 0) < v:
                        h.wait_ge(self.sems[s], v)
                        waited[s] = v
                        self.nins += 1
                ins = op.fn(h)
                self.nins += 1
                if op.is_dma:
                    ins.then_inc(self.sems[op.dsem], 16)
                elif op.signal:
                    ins.then_inc(self.sems[self.esem[e]], 1)
            self.ops[e] = []

    def barrier(self):
        deps = [self.last[e] for e in ENGS if self.last[e] is not None] + list(self.dmas)
        join = self.add("sp", lambda h: h.nop(), deps=deps)
        for e in ENGS:
            if e != "sp":
                self.add(e, lambda h: h.nop(), deps=[join])
        self.dmas = []
        self.flush()


def pieces(w, step=512):
    out = []
    c = 0
    while c < w:
        out.append((c, min(step, w - c)))
        c += step
    return out


class Builder:
    def __init__(self, NT, debug=False):
        self.NT = NT
        self.NTP = NT + 2 * PAD
        self.n512 = NT // T
        self.debug = debug
        self.NPAR = PC_FLAG + 2 * self.n512
        self._uid = 0
        nc = self.nc = bass.Bass("TRN2", target_bir_lowering=False)

        def din(name, shape, dt=F32):
            return nc.dram_tensor(name, shape, dt, kind="ExternalInput").ap()

        def dscr(name, shape, dt=F32):
            if debug:
                return nc.dram_tensor(name, shape, dt, kind="ExternalOutput").ap()
            return nc.dram_tensor(name, shape, dt).ap()

        self.xT = din("xT", [D, NT])
        self.par_d = din("par", [128, self.NPAR])
        self.cst_d = din("cst", [128, 512])
        self.w = {}
        for nm, shp in (("ffn1_gate", [D, DFF]), ("ffn1_up", [D, DFF]), ("ffn1_down", [DFF, D]),
                        ("w_in", [D, 5632]),
                        ("lru_wa_f", [1024, 128]), ("lru_wi_f", [1024, 128]),
                        ("lru_wa_b", [1024, 128]), ("lru_wi_b", [1024, 128]),
                        ("w_attn_o", [D, D]), ("w_rnn_o", [D, D]), ("w_out", [D, D]),
                        ("ffn2_gate", [D, DFF]), ("ffn2_up", [D, DFF]), ("ffn2_down", [DFF, D])):
            self.w[nm] = din(nm, shp)
        self.yT = nc.dram_tensor("yT", [D, NT], F32, kind="ExternalOutput").ap()
        self.x1_s = dscr("x1_s", [D, self.NTP])
        self.xr_s = dscr("xr_s", [D, self.NTP])
        self.gy_s = dscr("gy_s", [D, NT])
        self.xcb_s = dscr("xcb_s", [D, NT], BF16)
        self.hb_s = dscr("hb_s", [D, NT])
        self.hy_s = dscr("hy_s", [D, NT], BF16)
        self.ot_s = dscr("ot_s", [D, NT], BF16)
        self.x2_s = dscr("x2_s", [D, NT])
        if debug:
            self.dbg = nc.dram_tensor("dbg", [8, 128, 2048], F32, kind="ExternalOutput").ap()

    def _sbt(self, name, shape, dt):
        self._uid += 1
        return self.nc.sbuf_tensor("sb%d_%s" % (self._uid, name), shape, dt)

    def _pst(self, name, shape, dt):
        self._uid += 1
        return self.nc.psum_tensor("pp%d_%s" % (self._uid, name), shape, dt)

    def dview(self, ap, c0, w):
        return ap[:, c0:c0 + w].rearrange("(k p) t -> p k t", p=128)

    def load_w(self, P, sb, name, src, kc, ncols, c0=0):
        bufs = []
        for k in range(kc):
            bl = []
            for (cc, cw) in pieces(ncols, 1408):
                b = Buf("%s%d_%d" % (name, k, cc), ro=True)
                bl.append(b)
                P.add("pool", (lambda h, k=k, cc=cc, cw=cw: h.dma_start(
                    out=sb[:, k, cc:cc + cw], in_=src[k * 128:(k + 1) * 128, c0 + cc:c0 + cc + cw])),
                    W=[b], dma=True)
            bufs.append(bl)
        return bufs

    def build(self, phases=("p0", "p1", "p2", "p3", "p4", "p5a", "p5b", "p6")):
        nc = self.nc
        with ExitStack() as es:
            P = self.P = Prog(nc, es)
            self.par = es.enter_context(self._sbt("par", [128, self.NPAR], F32))
            self.ones_s = es.enter_context(self._sbt("ones_s", [128, 128], BF16))
            self.b_par = Buf("par", ro=True)
            self.b_ones = Buf("ones", ro=True)
            par, ones_s = self.par, self.ones_s
            P.add("sp", lambda h: h.dma_start(out=par[:], in_=self.par_d[:]), W=[self.b_par], dma=True)
            P.add("dve", lambda h: h.memset(ones_s[:], 1.0 / D), W=[self.b_ones])
            self.epsc = es.enter_context(self._sbt("epsc", [128, 1], F32))
            epsc = self.epsc
            P.add("dve", lambda h: h.memset(epsc[:], EPS), W=[self.b_ones])
            if "p0" in phases:
                self.phase0()
            if "p1" in phases:
                self.ffn_phase(1)
            if "p2" in phases:
                self.phase2()
            if "p3" in phases:
                self.rnn_phase(fwd=False)
            if "p4" in phases:
                self.rnn_phase(fwd=True)
            if "p5a" in phases:
                self.phase5a()
            if "p5b" in phases:
                self.phase5b()
            if "p6" in phases:
                self.ffn_phase(2)
            P.barrier()
        return nc

    def pcol(self, c):
        return self.par[:, c:c + 1]

    def phase0(self):
        nc, P = self.nc, self.P
        with ExitStack() as es:
            z = es.enter_context(self._sbt("zpad", [128, KC, PAD], F32))
            bz = Buf("z")
            P.add("dve", lambda h: h.memset(z[:], 0.0), W=[bz])
            for scr in (self.x1_s, self.xr_s):
                for c0 in (0, PAD + self.NT):
                    P.add("sp", (lambda h, scr=scr, c0=c0: h.dma_start(out=self.dview(scr, c0, PAD), in_=z[:])),
                          R=[bz], dma=True)
            P.barrier()

    def norm_full(self, P, xt, bx, W, gbase, sq, bsq, ps_list, rstd, brstd, hout, bh, stage=None):
        ones_s = self.ones_s
        if stage in (None, "sq"):
            for k in range(KC):
                P.add("act", (lambda h, k=k: h.activation(out=sq[:, k, 0:W], in_=xt[:, k, 0:W], func=AF.Square)),
                      R=[bx], W=[bsq], acc=(k > 0))
        if stage in (None, "ms"):
            for pi, (c0, cw) in enumerate(pieces(W)):
                ps, bps = ps_list[pi % len(ps_list)]
                for k in range(KC):
                    P.mm(ps[:, 0:cw], ones_s[:], sq[:, k, c0:c0 + cw], (k == 0), (k == KC - 1),
                         R=[bsq, self.b_ones], W=[bps])
                P.add("act", (lambda h, c0=c0, cw=cw, ps=ps: h.activation(
                    out=rstd[:, c0:c0 + cw], in_=ps[:, 0:cw], func=AF.Ln, bias=self.epsc[:], scale=1.0)),
                    R=[bps, self.b_ones], W=[brstd])
                P.add("act", (lambda h, c0=c0, cw=cw: h.activation(
                    out=rstd[:, c0:c0 + cw], in_=rstd[:, c0:c0 + cw], func=AF.Exp, scale=-0.5)),
                    R=[brstd], W=[brstd])
        if stage is None:
            k0, k1 = 0, KC
        elif isinstance(stage, tuple):
            k0, k1 = stage[1], stage[2]
        else:
            return
        for k in range(k0, k1):
            P.add("dve", (lambda h, k=k: h.scalar_tensor_tensor(
                out=hout[:, k, 0:W], in0=xt[:, k, 0:W], scalar=self.pcol(gbase + k), in1=rstd[:, 0:W],
                op0=ALU.mult, op1=ALU.mult)), R=[bx, brstd, self.b_par, bsq], W=[bh], acc=(k > 0))

    def ffn_phase(self, which):
        nc, P = self.nc, self.P
        final = (which == 2)
        if which == 1:
            wg_d, wu_d, wd_d = self.w["ffn1_gate"], self.w["ffn1_up"], self.w["ffn1_down"]
            gbase = PC_G1
            src, src_off = self.xT, 0
            dst, dst_off = self.x1_s, PAD
        else:
            wg_d, wu_d, wd_d = self.w["ffn2_gate"], self.w["ffn2_up"], self.w["ffn2_down"]
            gbase = PC_G2
            src, src_off = self.x2_s, 0
            dst, dst_off = self.yT, 0
        ntiles = self.n512
        with ExitStack() as es:
            wg = es.enter_context(self._sbt("wg", [128, KC, DFF], BF16))
            wu = es.enter_context(self._sbt("wu", [128, KC, DFF], BF16))
            wd = es.enter_context(self._sbt("wd", [128, FC, D], BF16))
            xts = [es.enter_context(self._sbt("xt%d" % i, [128, KC, T], F32)) for i in range(2)]
            xn = es.enter_context(self._sbt("xn", [128, KC, T], BF16))
            hid = es.enter_context(self._sbt("hid", [128, FC, T], BF16))
            sgs = [es.enter_context(self._sbt("sg%d" % i, [128, T], F32)) for i in range(2)]
            sqc = [es.enter_context(self._sbt("sqc%d" % i, [128, T], BF16)) for i in range(2)]
            sqf = [es.enter_context(self._sbt("sqf%d" % i, [128, T], BF16)) for i in range(2 if final else 0)]
            b_sqf = [Buf("sqf0"), Buf("sqf1")]
            sqf_ctr = [0]
            rstd = es.enter_context(self._sbt("rstd", [128, T], F32))
            rstdF = es.enter_context(self._sbt("rstdF", [128, T], F32))
            psum = es.enter_context(self._pst("psf", [128, 8, T], F32))
            b_xt = [[Buf("xt%d_%d" % (i, k)) for k in range(KC)] for i in range(2)]
            b_xn = [Buf("xn%d" % k) for k in range(KC)]
            b_hid = [Buf("hid%d" % j) for j in range(FC)]
            b_sg = [Buf("sg0"), Buf("sg1")]
            b_sqc = [Buf("sqc0"), Buf("sqc1")]
            b_rstd, b_rstdF = Buf("rstd"), Buf("rstdF")
            b_ps = [Buf("ps%d" % i) for i in range(8)]
            PS_G, PS_U, PS_Y, PS_MS, PS_MF = (0, 1), (2, 3), (4, 5), 6, 7

            b_wg = self.load_w(P, wg, "wg", wg_d, KC, DFF)
            b_wu = self.load_w(P, wu, "wu", wu_d, KC, DFF)
            b_wd = self.load_w(P, wd, "wd", wd_d, FC, D)

            hooks = deque()

            def group_done():
                if hooks:
                    hooks.popleft()()

            def load(i):
                xt = xts[i % 2]
                P.add("sp", lambda h: h.dma_start(out=xt[:], in_=self.dview(src, src_off + i * T, T)),
                      W=b_xt[i % 2], dma=True)

            sq_ctr = [0]

            def norm_parts(i):
                xt = xts[i % 2]
                bx = b_xt[i % 2]

                def p_sq():
                    for k in range(KC):
                        P.add("act", (lambda h, k=k: h.activation(out=xn[:, k, :], in_=xt[:, k, :], func=AF.Square)),
                              R=[bx[k]], W=[b_xn[k]])

                def p_ms():
                    for k in range(KC):
                        P.mm(psum[:, PS_MS, :], self.ones_s[:], xn[:, k, :], (k == 0), (k == KC - 1),
                             R=[b_xn[k], self.b_ones], W=[b_ps[PS_MS]])
                    P.add("act", lambda h: h.activation(out=rstd[:], in_=psum[:, PS_MS, :], func=AF.Ln,
                                                        bias=self.epsc[:], scale=1.0),
                          R=[b_ps[PS_MS], self.b_ones], W=[b_rstd])
                    P.add("act", lambda h: h.activation(out=rstd[:], in_=rstd[:], func=AF.Exp, scale=-0.5),
                          R=[b_rstd], W=[b_rstd])

                def p_xn(k0, k1):
                    for k in range(k0, k1):
                        P.add("dve", (lambda h, k=k: h.scalar_tensor_tensor(
                            out=xn[:, k, :], in0=xt[:, k, :], scalar=self.pcol(gbase + k), in1=rstd[:],
                            op0=ALU.mult, op1=ALU.mult)), R=[bx[k], b_rstd, self.b_par, b_xn[k]], W=[b_xn[k]])

                return [p_sq, p_ms, (lambda: p_xn(0, 4)), (lambda: p_xn(4, 8))]

            def gateup(i):
                for j in range(FC):
                    pg, pu = PS_G[j % 2], PS_U[j % 2]
                    for k in range(KC):
                        P.mm(psum[:, pg, :], wg[:, k, j * 128:(j + 1) * 128], xn[:, k, :], (k == 0), (k == KC - 1),
                             R=b_wg[k] + [b_xn[k]], W=[b_ps[pg]])
                    for k in range(KC):
                        P.mm(psum[:, pu, :], wu[:, k, j * 128:(j + 1) * 128], xn[:, k, :], (k == 0), (k == KC - 1),
                             R=b_wu[k] + [b_xn[k]], W=[b_ps[pu]])
                    s = j % 2
                    P.add("act", (lambda h, pg=pg, s=s: h.activation(out=sgs[s][:], in_=psum[:, pg, :], func=AF.Silu)),
                          R=[b_ps[pg]], W=[b_sg[s]])
                    P.add("dve", (lambda h, j=j, pu=pu, s=s: h.tensor_tensor(
                        out=hid[:, j, :], in0=sgs[s][:], in1=psum[:, pu, :], op=ALU.mult)),
                        R=[b_sg[s], b_ps[pu]], W=[b_hid[j]])
                    group_done()

            def final_rstd(i):
                P.add("act", lambda h: h.activation(out=rstdF[:], in_=psum[:, PS_MF, :], func=AF.Ln, bias=self.epsc[:], scale=1.0),
                      R=[b_ps[PS_MF], self.b_ones], W=[b_rstdF])
                P.add("act", lambda h: h.activation(out=rstdF[:], in_=rstdF[:], func=AF.Exp, scale=-0.5),
                      R=[b_rstdF], W=[b_rstdF])

            def final_scale(i, k0, k1, last):
                xt = xts[i % 2]
                bx = b_xt[i % 2]
                for k in range(k0, k1):
                    P.add("dve", (lambda h, k=k: h.scalar_tensor_tensor(
                        out=xt[:, k, :], in0=xt[:, k, :], scalar=self.pcol(PC_GF + k), in1=rstdF[:],
                        op0=ALU.mult, op1=ALU.mult)), R=[bx[k], b_rstdF, self.b_par], W=[bx[k]])
                if last:
                    store(i)

            def store(i):
                xt = xts[i % 2]
                P.add("sp", lambda h: h.dma_start(out=self.dview(dst, dst_off + i * T, T), in_=xt[:]),
                      R=b_xt[i % 2], dma=True)
                if i + 2 < ntiles:
                    load(i + 2)

            def down(i, nparts=()):
                xt = xts[i % 2]
                bx = b_xt[i % 2]
                for m in range(KC):
                    py = PS_Y[m % 2]
                    for j in range(FC):
                        P.mm(psum[:, py, :], wd[:, j, m * 128:(m + 1) * 128], hid[:, j, :], (j == 0), (j == FC - 1),
                             R=b_wd[j] + [b_hid[j]], W=[b_ps[py]])
                    P.add("dve", (lambda h, m=m, py=py: h.scalar_tensor_tensor(
                        out=xt[:, m, :], in0=psum[:, py, :], scalar=0.5, in1=xt[:, m, :],
                        op0=ALU.mult, op1=ALU.add)), R=[b_ps[py], bx[m]], W=[bx[m]])
                    if final:
                        s = sqf_ctr[0] % 2
                        sqf_ctr[0] += 1
                        P.add("act", (lambda h, m=m, s=s: h.activation(out=sqf[s][:], in_=xt[:, m, :], func=AF.Square)),
                              R=[bx[m]], W=[b_sqf[s]])

                        def msf(m=m, s=s):
                            P.mm(psum[:, PS_MF, :], self.ones_s[:], sqf[s][:], (m == 0), (m == KC - 1),
                                 R=[b_sqf[s], self.b_ones], W=[b_ps[PS_MF]])
                        hooks.append(msf)
                    group_done()
                    if m < len(nparts):
                        nparts[m]()
                if final:
                    hooks.append(lambda: final_rstd(i))
                    for q in range(4):
                        hooks.append(lambda q=q: final_scale(i, 2 * q, 2 * q + 2, q == 3))
                else:
                    store(i)

            load(0)
            if ntiles > 1:
                load(1)
            for part in norm_parts(0):
                part()
            for i in range(ntiles):
                gateup(i)
                down(i, norm_parts(i + 1) if i + 1 < ntiles else ())
            while hooks:
                hooks.popleft()()
            P.barrier()

    def phase2(self):
        nc, P = self.nc, self.P
        ntiles = self.n512
        WH = T + 3
        WA = T + 4
        with ExitStack() as es:
            wxy = es.enter_context(self._sbt("wxy", [128, KC, 2048], BF16))
            xts = [es.enter_context(self._sbt("xt%d" % i, [128, KC, WA], F32)) for i in range(2)]
            hts = [es.enter_context(self._sbt("ht%d" % i, [128, KC, WA], BF16)) for i in range(2)]
            rstds = [es.enter_context(self._sbt("rstd%d" % i, [128, WA], F32)) for i in range(2)]
            xro = [es.enter_context(self._sbt("xro%d" % i, [128, KC, WA], F32)) for i in range(2)]
            xco = [es.enter_context(self._sbt("xco%d" % i, [128, KC, T], F32)) for i in range(2)]
            gyo = [es.enter_context(self._sbt("gyo%d" % i, [128, KC, T], F32)) for i in range(2)]
            xcb = [es.enter_context(self._sbt("xcb%d" % i, [128, KC, T], BF16)) for i in range(2)]
            b_xcb = [Buf("xcb0"), Buf("xcb1")]
            psum = es.enter_context(self._pst("ps2", [128, 8, T], F32))
            b_xt = [Buf("xt0"), Buf("xt1")]
            b_ht = [Buf("ht0"), Buf("ht1")]
            b_rstd = [Buf("r0"), Buf("r1")]
            b_xro = [[Buf("xro") for _ in range(KC)] for _ in range(2)]
            b_xco = [[Buf("xco") for _ in range(KC)] for _ in range(2)]
            b_gyo = [Buf("gyo0"), Buf("gyo1")]
            b_ps = [Buf("ps%d" % i) for i in range(8)]
            b_w = self.load_w(P, wxy, "wxy", self.w["w_in"], KC, 2048, c0=1536)

            def load(i):
                P.add("sp", lambda h: h.dma_start(out=xts[i % 2][:, :, 0:WH], in_=self.dview(self.x1_s, PAD + i * T - 2, WH)),
                      W=[b_xt[i % 2]], dma=True)

            def norm(i):
                s = i % 2
                self.norm_full(P, xts[s], b_xt[s], WH, PC_GM, hts[s], b_ht[s],
                               [(psum[:, 6, :], b_ps[6]), (psum[:, 7, :], b_ps[7])],
                               rstds[s], b_rstd[s], hts[s], b_ht[s])

            pctr = [0]

            def proj(i):
                s = i % 2
                ht = hts[s]
                fL = PC_FLAG + i
                fR = PC_FLAG + self.n512 + i
                bxr = b_xro[s][0]
                for m in range(8):
                    for (c0, cw) in pieces(WH):
                        pb = pctr[0] % 6
                        pctr[0] += 1
                        for k in range(KC):
                            P.mm(psum[:, pb, 0:cw], wxy[:, k, m * 128:(m + 1) * 128], ht[:, k, c0:c0 + cw],
                                 (k == 0), (k == KC - 1), R=b_w[k] + [b_ht[s]], W=[b_ps[pb]])
                        P.add("act", (lambda h, m=m, pb=pb, c0=c0, cw=cw: h.activation(
                            out=xro[s][:, m, c0:c0 + cw], in_=psum[:, pb, 0:cw], func=AF.Copy)),
                            R=[b_ps[pb]], W=[bxr])
                P.add("dve", (lambda h: h.tensor_scalar(
                    out=xro[s][:, :, 0:2], in0=xro[s][:, :, 0:2], scalar1=self.pcol(fL), scalar2=None, op0=ALU.mult)),
                    R=[bxr, self.b_par], W=[bxr])
                P.add("dve", (lambda h: h.tensor_scalar(
                    out=xro[s][:, :, T + 2:T + 3], in0=xro[s][:, :, T + 2:T + 3], scalar1=self.pcol(fR), scalar2=None,
                    op0=ALU.mult)), R=[bxr, self.b_par], W=[bxr])

                def conv(m):
                    P.add("dve", (lambda h: h.tensor_scalar(
                        out=xco[s][:, m, :], in0=xro[s][:, m, 0:T], scalar1=self.pcol(PC_CW + m), scalar2=self.pcol(PC_CB + m),
                        op0=ALU.mult, op1=ALU.add)), R=[bxr, self.b_par], W=[b_xco[s][m]])
                    for tap in range(1, 4):
                        P.add("dve", (lambda h, tap=tap: h.scalar_tensor_tensor(
                            out=xco[s][:, m, :], in0=xro[s][:, m, tap:tap + T], scalar=self.pcol(PC_CW + tap * 8 + m),
                            in1=xco[s][:, m, :], op0=ALU.mult, op1=ALU.add)),
                            R=[bxr, self.b_par, b_xco[s][m]], W=[b_xco[s][m]])
                    P.add("pool", (lambda h: h.tensor_copy(out=xcb[s][:, m, :], in_=xco[s][:, m, :])),
                          R=[b_xco[s][m]], W=[b_xcb[s]])

                for m in range(8):
                    pb = pctr[0] % 6
                    pctr[0] += 1
                    for k in range(KC):
                        P.mm(psum[:, pb, :], wxy[:, k, 1024 + m * 128:1024 + (m + 1) * 128], ht[:, k, 2:2 + T],
                             (k == 0), (k == KC - 1), R=b_w[k] + [b_ht[s]], W=[b_ps[pb]])
                    P.add("act", (lambda h, m=m, pb=pb: h.activation(out=gyo[s][:, m, :], in_=psum[:, pb, :],
                                                                     func=AF.Gelu_apprx_tanh)),
                          R=[b_ps[pb]], W=[b_gyo[s]])
                    conv(m)
                    if m == 3 and i + 1 < ntiles:
                        norm(i + 1)
                P.add("sp", lambda h: h.dma_start(out=self.dview(self.xr_s, PAD + i * T, T), in_=xco[s][:]),
                      R=b_xco[s], dma=True)
                P.add("sp", lambda h: h.dma_start(out=self.dview(self.gy_s, i * T, T), in_=gyo[s][:]),
                      R=[b_gyo[s]], dma=True)
                P.add("sp", lambda h: h.dma_start(out=self.dview(self.xcb_s, i * T, T), in_=xcb[s][:]),
                      R=[b_xcb[s]], dma=True)
                if i + 2 < ntiles:
                    load(i + 2)

            load(0)
            if ntiles > 1:
                load(1)
            norm(0)
            for i in range(ntiles):
                proj(i)
            P.barrier()

    def rnn_phase(self, fwd):
        nc, P = self.nc, self.P
        RT = 2048
        ntl = self.NT // RT
        tiles = list(range(ntl)) if fwd else list(range(ntl - 1, -1, -1))
        wa_d = self.w["lru_wa_f" if fwd else "lru_wa_b"]
        wi_d = self.w["lru_wi_f" if fwd else "lru_wi_b"]
        PBA, PBI, PLAM = (PC_BAF, PC_BIF, PC_LAMF) if fwd else (PC_BAB, PC_BIB, PC_LAMB)
        with ExitStack() as es:
            wa = es.enter_context(self._sbt("wa", [128, KC, 128], BF16))
            wi = es.enter_context(self._sbt("wi", [128, KC, 128], BF16))
            cc = es.enter_context(self._sbt("cc", [128, 3 * KC], F32))
            carry = es.enter_context(self._sbt("carry", [128, KC], F32))
            NB = 2
            xc = [es.enter_context(self._sbt("xc%d" % i, [128, RT], F32)) for i in range(NB)]
            xcb = [es.enter_context(self._sbt("xcb%d" % i, [128, RT], BF16)) for i in range(NB)]
            b_xcb = [Buf("xcb") for _ in range(NB)]
            rr = [es.enter_context(self._sbt("rr%d" % i, [128, RT], F32)) for i in range(NB)]
            ii = [es.enter_context(self._sbt("ii%d" % i, [128, RT], F32)) for i in range(NB)]
            aa = [es.enter_context(self._sbt("aa%d" % i, [128, RT], F32)) for i in range(NB)]
            a2 = [es.enter_context(self._sbt("a2%d" % i, [128, RT], F32)) for i in range(NB)]
            hh = [es.enter_context(self._sbt("hh%d" % i, [128, RT], F32)) for i in range(NB)]
            if fwd:
                hbt = [es.enter_context(self._sbt("hbt%d" % i, [128, RT], F32)) for i in range(NB)]
                gyt = [es.enter_context(self._sbt("gyt%d" % i, [128, RT], F32)) for i in range(NB)]
                hyt = [es.enter_context(self._sbt("hyt%d" % i, [128, RT], BF16)) for i in range(NB)]
                b_hbt = [Buf("hbt") for _ in range(NB)]
                b_gyt = [Buf("gyt") for _ in range(NB)]
                b_hyt = [Buf("hyt") for _ in range(NB)]
            psum = es.enter_context(self._pst("ps3", [128, 8, T], F32))
            b_wa, b_wi = Buf("wa", ro=True), Buf("wi", ro=True)
            b_cc, b_carry = Buf("cc"), Buf("carry")
            b_xc = [Buf("xc") for _ in range(NB)]
            b_rr = [Buf("rr") for _ in range(NB)]
            b_ii = [Buf("ii") for _ in range(NB)]
            b_aa = [Buf("aa") for _ in range(NB)]
            b_a2 = [Buf("a2") for _ in range(NB)]
            b_hh = [Buf("hh") for _ in range(NB)]
            b_ps = [Buf("ps%d" % i) for i in range(8)]
            P.add("pool", lambda h: h.dma_start(out=wa[:], in_=wa_d.rearrange("(k p) d -> p k d", p=128)),
                  W=[b_wa], dma=True)
            P.add("pool", lambda h: h.dma_start(out=wi[:], in_=wi_d.rearrange("(k p) d -> p k d", p=128)),
                  W=[b_wi], dma=True)
            P.add("act", lambda h: h.activation(out=cc[:, 16:24], in_=self.par[:, PLAM:PLAM + 8], func=AF.Exp, scale=-1.0),
                  R=[self.b_par], W=[b_cc])
            P.add("act", lambda h: h.activation(out=cc[:, 16:24], in_=cc[:, 16:24], func=AF.Ln, bias=1.0, scale=1.0),
                  R=[b_cc, self.b_ones], W=[b_cc])
            P.add("dve", lambda h: h.tensor_scalar(out=cc[:, 0:8], in0=cc[:, 16:24], scalar1=-8.0, scalar2=None, op0=ALU.mult),
                  R=[b_cc], W=[b_cc])
            P.add("dve", lambda h: h.tensor_scalar(out=cc[:, 8:16], in0=cc[:, 16:24], scalar1=-16.0, scalar2=None, op0=ALU.mult),
                  R=[b_cc], W=[b_cc])
            P.add("dve", lambda h: h.memset(carry[:], 0.0), W=[b_carry])

            items = [(ti, n) for ti in tiles for n in range(KC)]

            def loads(idx):
                ti, n = items[idx]
                s = idx % NB
                t0 = ti * RT
                rows = slice(n * 128, (n + 1) * 128)
                P.add("sp", (lambda h: h.dma_start(out=xc[s][:], in_=self.xr_s[rows, PAD + t0:PAD + t0 + RT])),
                      W=[b_xc[s]], dma=True)
                P.add("sp", (lambda h: h.dma_start(out=xcb[s][:], in_=self.xcb_s[rows, t0:t0 + RT])),
                      W=[b_xcb[s]], dma=True)
                if fwd:
                    P.add("sp", (lambda h: h.dma_start(out=hbt[s][:], in_=self.hb_s[rows, t0:t0 + RT])),
                          W=[b_hbt[s]], dma=True)
                    P.add("sp", (lambda h: h.dma_start(out=gyt[s][:], in_=self.gy_s[rows, t0:t0 + RT])),
                          W=[b_gyt[s]], dma=True)

            loads(0)
            pc = [0]

            def compute(idx):
                ti, n = items[idx]
                t0 = ti * RT
                fL = PC_FLAG + (t0 // T)
                fR = PC_FLAG + self.n512 + (t0 + RT) // T - 1
                s = idx % NB
                rows = slice(n * 128, (n + 1) * 128)
                if idx + 1 < len(items):
                    loads(idx + 1)
                for (c0, cw) in pieces(RT):
                    pa = pc[0] % 8
                    pb = (pc[0] + 1) % 8
                    pc[0] += 2
                    P.mm(psum[:, pa, :], wa[:, n, :], xcb[s][:, c0:c0 + T], True, True, R=[b_wa, b_xcb[s]], W=[b_ps[pa]])
                    P.mm(psum[:, pb, :], wi[:, n, :], xcb[s][:, c0:c0 + T], True, True, R=[b_wi, b_xcb[s]], W=[b_ps[pb]])
                    P.add("act", (lambda h, c0=c0, pa=pa: h.activation(
                        out=rr[s][:, c0:c0 + T], in_=psum[:, pa, :], func=AF.Sigmoid, bias=self.pcol(PBA + n), scale=1.0)),
                        R=[b_ps[pa], self.b_par], W=[b_rr[s]])
                    P.add("act", (lambda h, c0=c0, pb=pb: h.activation(
                        out=ii[s][:, c0:c0 + T], in_=psum[:, pb, :], func=AF.Sigmoid, bias=self.pcol(PBI + n), scale=1.0)),
                        R=[b_ps[pb], self.b_par], W=[b_ii[s]])
                P.add("act", (lambda h: h.activation(out=aa[s][:], in_=rr[s][:], func=AF.Exp, scale=cc[:, n:n + 1])),
                      R=[b_rr[s], b_cc], W=[b_aa[s]])
                P.add("act", (lambda h: h.activation(out=a2[s][:], in_=rr[s][:], func=AF.Exp, scale=cc[:, 8 + n:9 + n])),
                      R=[b_rr[s], b_cc], W=[b_a2[s]])
                P.add("act", (lambda h: h.activation(out=a2[s][:], in_=a2[s][:], func=AF.Sqrt, bias=1.0, scale=-1.0)),
                      R=[b_a2[s], self.b_ones], W=[b_a2[s]])
                P.add("dve", (lambda h: h.tensor_tensor(out=ii[s][:], in0=ii[s][:], in1=xc[s][:], op=ALU.mult)),
                      R=[b_ii[s], b_xc[s]], W=[b_ii[s]])
                P.add("pool", (lambda h: h.tensor_tensor(out=ii[s][:], in0=ii[s][:], in1=a2[s][:], op=ALU.mult)),
                      R=[b_ii[s], b_a2[s]], W=[b_ii[s]])
                if self.debug and idx == 0 and not fwd:
                    for di, (tt, bb) in enumerate(((xc[s], b_xc[s]), (rr[s], b_rr[s]), (aa[s], b_aa[s]), (a2[s], b_a2[s]), (ii[s], b_ii[s]))):
                        P.add("sp", (lambda h, di=di, tt=tt: h.dma_start(out=self.dbg[di], in_=tt[:])), R=[bb], dma=True)
                if fwd:
                    P.add("dve", (lambda h: h.tensor_tensor_scan(
                        out=hh[s][:], data0=aa[s][:], data1=ii[s][:], initial=carry[:, n:n + 1],
                        op0=ALU.mult, op1=ALU.add)), R=[b_aa[s], b_ii[s], b_carry], W=[b_hh[s]])
                    P.add("dve", (lambda h: h.tensor_scalar(
                        out=carry[:, n:n + 1], in0=hh[s][:, RT - 1:RT], scalar1=self.pcol(fR), scalar2=None, op0=ALU.mult)),
                        R=[b_hh[s], self.b_par], W=[b_carry])
                    P.add("dve", (lambda h: h.tensor_tensor(out=hh[s][:], in0=hh[s][:], in1=hbt[s][:], op=ALU.add)),
                          R=[b_hh[s], b_hbt[s]], W=[b_hh[s]])
                    P.add("pool", (lambda h: h.tensor_tensor(out=hyt[s][:], in0=hh[s][:], in1=gyt[s][:], op=ALU.mult)),
                          R=[b_hh[s], b_gyt[s]], W=[b_hyt[s]])
                    P.add("sp", (lambda h: h.dma_start(out=self.hy_s[rows, t0:t0 + RT], in_=hyt[s][:])),
                          R=[b_hyt[s]], dma=True)
                else:
                    P.add("dve", (lambda h: h.tensor_tensor_scan(
                        out=hh[s][:, ::-1], data0=aa[s][:, ::-1], data1=ii[s][:, ::-1], initial=carry[:, n:n + 1],
                        op0=ALU.mult, op1=ALU.add)), R=[b_aa[s], b_ii[s], b_carry], W=[b_hh[s]])
                    P.add("dve", (lambda h: h.tensor_scalar(
                        out=carry[:, n:n + 1], in0=hh[s][:, 0:1], scalar1=self.pcol(fL), scalar2=None, op0=ALU.mult)),
                        R=[b_hh[s], self.b_par], W=[b_carry])
                    P.add("sp", (lambda h: h.dma_start(out=self.hb_s[rows, t0:t0 + RT], in_=hh[s][:])),
                          R=[b_hh[s]], dma=True)

            for idx in range(len(items)):
                compute(idx)
            P.barrier()

    def phase5a(self):
        nc, P = self.nc, self.P
        ntiles = self.n512
        W = T + 2 * PAD
        with ExitStack() as es:
            wq = es.enter_context(self._sbt("wq", [128, KC, 1024], BF16))
            wkv = es.enter_context(self._sbt("wkv", [128, KC, 512], BF16))
            cst = es.enter_context(self._sbt("cst", [128, 512], F32))
            ident = es.enter_context(self._sbt("ident", [128, 128], BF16))
            bias = es.enter_context(self._sbt("bias", [128, 8, 384], F32))
            nsink = es.enter_context(self._sbt("nsink", [128, 8], F32))
            onesr = es.enter_context(self._sbt("onesr", [1, 128], BF16))
            penr = [es.enter_context(self._sbt("penr%d" % i, [1, 2, 128], BF16)) for i in range(2)]
            penf = [es.enter_context(self._sbt("penf%d" % i, [1, 2], F32)) for i in range(2)]
            xh = [es.enter_context(self._sbt("xh%d" % i, [128, KC, W], F32)) for i in range(2)]
            ht = [es.enter_context(self._sbt("ht%d" % i, [128, KC, W], BF16)) for i in range(2)]
            rstd = [es.enter_context(self._sbt("rstd%d" % i, [128, W], F32)) for i in range(2)]
            qT = [es.enter_context(self._sbt("qT%d" % i, [128, 8, T], BF16)) for i in range(2)]
            kT = [es.enter_context(self._sbt("kT%d" % i, [128, 2, W], BF16)) for i in range(2)]
            vv = [es.enter_context(self._sbt("vv%d" % i, [128, 6, 256], BF16)) for i in range(2)]
            ss = [es.enter_context(self._sbt("ss%d" % i, [128, 4, 384], F32)) for i in range(2)]
            pp = [es.enter_context(self._sbt("pp%d" % i, [128, 4, 384], F32)) for i in range(2)]
            pn = [es.enter_context(self._sbt("pn%d" % i, [128, 4, 384], BF16)) for i in range(2)]
            pT = [es.enter_context(self._sbt("pT%d" % i, [128, 4, 384], BF16)) for i in range(2)]
            st = [es.enter_context(self._sbt("st%d" % i, [128, 32], F32)) for i in range(2)]
            oT = [es.enter_context(self._sbt("oT%d" % i, [128, 8, T], BF16)) for i in range(2)]
            ps_s = es.enter_context(self._pst("ps_s", [128, 4, T], F32))
            ps_t = es.enter_context(self._pst("ps_t", [128, 2, 1024], BF16))
            ps_o = es.enter_context(self._pst("ps_o", [128, 4, 128], F32))
            ps_m = es.enter_context(self._pst("ps_m", [128, T], F32))
            b_cst = Buf("cst", ro=True)
            b_ident, b_bias, b_nsink, b_onesr = Buf("ident", ro=True), Buf("bias", ro=True), Buf("nsink", ro=True), Buf("onesr", ro=True)
            b_penr, b_penf = [Buf("penr0"), Buf("penr1")], [Buf("penf0"), Buf("penf1")]
            b_xh = [Buf("xh0"), Buf("xh1")]
            b_ht, b_rstd = [Buf("ht0"), Buf("ht1")], [Buf("rstd0"), Buf("rstd1")]
            b_qT, b_kT, b_vv = [Buf("qT0"), Buf("qT1")], [Buf("kT0"), Buf("kT1")], [Buf("vv0"), Buf("vv1")]
            b_ss = [Buf("ss0"), Buf("ss1")]
            b_pp = [Buf("pp0"), Buf("pp1")]
            b_pn = [[Buf("pn0a"), Buf("pn0b")], [Buf("pn1a"), Buf("pn1b")]]
            b_pT = [[Buf("pT0a"), Buf("pT0b")], [Buf("pT1a"), Buf("pT1b")]]
            b_stA = [Buf("stA0"), Buf("stA1")]
            b_stB = [Buf("stB0"), Buf("stB1")]
            b_stC = [Buf("stC0"), Buf("stC1")]
            b_stD = [Buf("stD0"), Buf("stD1")]
            b_oT = [Buf("oT0"), Buf("oT1")]
            b_pss = [Buf("pss%d" % i) for i in range(4)]
            b_pst = [Buf("pst0"), Buf("pst1")]
            b_pso, b_psm = Buf("pso"), Buf("psm")

            b_wq = self.load_w(P, wq, "wq", self.w["w_in"], KC, 1024, c0=0)
            b_wkv = self.load_w(P, wkv, "wkv", self.w["w_in"], KC, 512, c0=1024)
            P.add("sp", lambda h: h.dma_start(out=cst[:], in_=self.cst_d[:]), W=[b_cst], dma=True)
            P.add("dve", lambda h: h.tensor_copy(out=ident[:], in_=cst[:, 0:128]), R=[b_cst], W=[b_ident])
            for hd in range(8):
                P.add("dve", (lambda h, hd=hd: h.tensor_scalar(out=bias[:, hd, :], in0=cst[:, 128:512],
                                                               scalar1=self.pcol(PC_SLOPE + hd), scalar2=None, op0=ALU.mult)),
                      R=[b_cst, self.b_par], W=[b_bias])
            P.add("dve", lambda h: h.tensor_scalar(out=nsink[:], in0=self.par[:, PC_SINK:PC_SINK + 8], scalar1=-1.0,
                                                   scalar2=None, op0=ALU.mult), R=[self.b_par], W=[b_nsink])
            P.add("dve", lambda h: h.memset(onesr[:], 1.0), W=[b_onesr])

            def load(i):
                P.add("sp", lambda h: h.dma_start(out=xh[i % 2][:], in_=self.dview(self.x1_s, i * T, W)),
                      W=[b_xh[i % 2]], dma=True)

            normed = set()

            def norm_parts(i):
                p = i % 2
                x, bx = xh[p], b_xh[p]
                h_, bh = ht[p], b_ht[p]
                r_, br = rstd[p], b_rstd[p]
                parts = []

                def sq(k0, k1):
                    for k in range(k0, k1):
                        P.add("act", (lambda h, k=k: h.activation(out=h_[:, k, 0:W], in_=x[:, k, 0:W], func=AF.Square)),
                              R=[bx], W=[bh], acc=(k > 0))
                parts.append(lambda: sq(0, 4))
                parts.append(lambda: sq(4, 8))

                def ms(c0, cw):
                    for k in range(KC):
                        P.mm(ps_m[:, 0:cw], self.ones_s[:], h_[:, k, c0:c0 + cw], (k == 0), (k == KC - 1),
                             R=[bh, self.b_ones], W=[b_psm])
                    P.add("act", (lambda h: h.activation(out=r_[:, c0:c0 + cw], in_=ps_m[:, 0:cw], func=AF.Ln,
                                                         bias=self.epsc[:], scale=1.0)), R=[b_psm, self.b_ones], W=[br])
                    P.add("act", (lambda h: h.activation(out=r_[:, c0:c0 + cw], in_=r_[:, c0:c0 + cw], func=AF.Exp,
                                                         scale=-0.5)), R=[br], W=[br])
                for (c0, cw) in pieces(W):
                    parts.append(lambda c0=c0, cw=cw: ms(c0, cw))

                def hop(k0, k1):
                    for k in range(k0, k1):
                        P.add("dve", (lambda h, k=k: h.scalar_tensor_tensor(
                            out=h_[:, k, 0:W], in0=x[:, k, 0:W], scalar=self.pcol(PC_GM + k), in1=r_[:, 0:W],
                            op0=ALU.mult, op1=ALU.mult)), R=[bx, br, self.b_par, bh], W=[bh], acc=(k > 0))
                for j in range(4):
                    parts.append(lambda j=j: hop(2 * j, 2 * j + 2))
                return parts

            def proj_groups(i):
                p = i % 2
                x, bx = xh[p], b_xh[p]
                h_, bh = ht[p], b_ht[p]
                G = deque()
                rot = [0]

                def bank():
                    r = rot[0] % 4
                    rot[0] += 1
                    return ps_s[:, r, :], b_pss[r]

                def g_norm():
                    fl = PC_FLAG + i
                    P.add("dve", lambda h: h.tensor_scalar(out=penf[p][:, 0:1], in0=self.par[0:1, fl:fl + 1], scalar1=1e30,
                                                           scalar2=-1e30, op0=ALU.mult, op1=ALU.add),
                          R=[self.b_par], W=[b_penf[p]])
                    P.add("dve", lambda h: h.tensor_scalar(out=penf[p][:, 1:2],
                                                           in0=self.par[0:1, fl + self.n512:fl + self.n512 + 1],
                                                           scalar1=1e30, scalar2=-1e30, op0=ALU.mult, op1=ALU.add),
                          R=[self.b_par], W=[b_penf[p]])
                    for sd in range(2):
                        P.add("dve", (lambda h, sd=sd: h.tensor_scalar(out=penr[p][:, sd, :], in0=onesr[:],
                                                                       scalar1=penf[p][:, sd:sd + 1], scalar2=None, op0=ALU.mult)),
                              R=[b_penf[p], b_onesr], W=[b_penr[p]])
                    if i not in normed:
                        self.norm_full(P, x, bx, W, PC_GM, h_, bh, [(ps_m, b_psm)], rstd[p], b_rstd[p], h_, bh)
                G.append(g_norm)

                def g_q(m):
                    pm, bpm = bank()
                    for k in range(KC):
                        P.mm(pm[:, :], wq[:, k, m * 128:(m + 1) * 128], h_[:, k, PAD:PAD + T],
                             (k == 0), (k == KC - 1), R=b_wq[k] + [bh], W=[bpm])
                    if m % 2 == 0:
                        P.add("act", (lambda h: h.activation(out=qT[p][:, m, :], in_=pm[:, :], func=AF.Copy, scale=SCALE)),
                              R=[bpm], W=[b_qT[p]])
                    else:
                        P.add("dve", (lambda h: h.tensor_scalar(out=qT[p][:, m, :], in0=pm[:, :], scalar1=SCALE,
                                                                scalar2=None, op0=ALU.mult)), R=[bpm], W=[b_qT[p]])
                for m in range(8):
                    G.append(lambda m=m: g_q(m))

                def g_k(g, c0, cw):
                    pm, bpm = bank()
                    for k in range(KC):
                        P.mm(pm[:, 0:cw], wkv[:, k, g * 128:(g + 1) * 128], h_[:, k, c0:c0 + cw],
                             (k == 0), (k == KC - 1), R=b_wkv[k] + [bh], W=[bpm])
                    P.add("dve", (lambda h: h.tensor_copy(out=kT[p][:, g, c0:c0 + cw], in_=pm[:, 0:cw])),
                          R=[bpm], W=[b_kT[p]])
                for g in range(2):
                    for (c0, cw) in pieces(W):
                        G.append(lambda g=g, c0=c0, cw=cw: g_k(g, c0, cw))

                def g_v(b):
                    pm, bpm = bank()
                    for k in range(KC):
                        P.mm(pm[:, 0:256], h_[:, k, b * 128:(b + 1) * 128], wkv[:, k, 256:512],
                             (k == 0), (k == KC - 1), R=b_wkv[k] + [bh], W=[bpm])
                    P.add("act", (lambda h: h.activation(out=vv[p][:, b, :], in_=pm[:, 0:256], func=AF.Copy)),
                          R=[bpm], W=[b_vv[p]])
                for b in range(6):
                    G.append(lambda b=b: g_v(b))
                return G

            def tile(i, nxt):
                p = i % 2
                o = oT[p]
                bo = b_oT[p]
                qT_, kT_, vv_, penr_ = qT[p], kT[p], vv[p], penr[p]
                bq, bk, bv, bpen = b_qT[p], b_kT[p], b_vv[p], b_penr[p]

                def slot():
                    if nxt:
                        nxt.popleft()()

                blocks = [(qb, g) for qb in range(4) for g in range(2)]

                def A(bi):
                    qb, g = blocks[bi]
                    has_pen = (qb == 0) or (qb == 3)
                    for hh in range(4):
                        hd = g * 4 + hh
                        P.mm(ps_s[:, hh, 0:384], qT_[:, hd, qb * 128:(qb + 1) * 128], kT_[:, g, qb * 128:qb * 128 + 384],
                             True, (not has_pen), R=[bq, bk], W=[b_pss[hh]])
                        if has_pen:
                            sd = 0 if qb == 0 else 1
                            cc0 = 0 if qb == 0 else 256
                            P.mm(ps_s[:, hh, cc0:cc0 + 128], onesr[:], penr_[:, sd, :], False, True,
                                 R=[b_onesr, bpen], W=[b_pss[hh]])

                def B1(bi):
                    qb, g = blocks[bi]
                    s = bi % 2
                    P.add("dve", (lambda h: h.tensor_tensor(
                        out=ss[s][:], in0=ps_s[:, :, 0:384], in1=bias[:, g * 4:(g + 1) * 4, :], op=ALU.add)),
                        R=b_pss + [b_bias], W=[b_ss[s]])

                def B2(bi):
                    qb, g = blocks[bi]
                    s = bi % 2
                    P.add("dve", (lambda h: h.tensor_reduce(out=st[s][:, 0:4], in_=ss[s][:], axis=AX.X, op=ALU.max)),
                          R=[b_ss[s]], W=[b_stA[s]])
                    P.add("dve", (lambda h: h.scalar_tensor_tensor(
                        out=st[s][:, 4:8], in0=st[s][:, 0:4], scalar=-1.0, in1=nsink[:, g * 4:(g + 1) * 4],
                        op0=ALU.mult, op1=ALU.min)), R=[b_stA[s], b_nsink], W=[b_stA[s]])
                    P.add("dve", (lambda h: h.tensor_tensor(
                        out=st[s][:, 24:28], in0=st[s][:, 4:8], in1=nsink[:, g * 4:(g + 1) * 4], op=ALU.subtract)),
                        R=[b_stA[s], b_nsink], W=[b_stA[s]])

                def C(bi):
                    s = bi % 2
                    P.add("act", (lambda h: h.activation(out=st[s][:, 12:16], in_=st[s][:, 24:28], func=AF.Exp)),
                          R=[b_stA[s]], W=[b_stC[s]])
                    for hh in range(4):
                        P.add("act", (lambda h, hh=hh: h.activation(
                            out=pp[s][:, hh, :], in_=ss[s][:, hh, :], func=AF.Exp, bias=st[s][:, 4 + hh:5 + hh], scale=1.0,
                            accum_out=st[s][:, 8 + hh:9 + hh])), R=[b_ss[s], b_stA[s]], W=[b_pp[s], b_stB[s]], acc=(hh > 0))

                def Dd(bi):
                    s = bi % 2
                    P.add("dve", (lambda h: h.tensor_tensor(
                        out=st[s][:, 16:20], in0=st[s][:, 8:12], in1=st[s][:, 12:16], op=ALU.add)),
                        R=[b_stB[s], b_stC[s]], W=[b_stD[s]])
                    P.add("dve", (lambda h: h.reciprocal(out=st[s][:, 20:24], in_=st[s][:, 16:20])),
                          R=[b_stD[s]], W=[b_stD[s]])

                def D2(bi):
                    s = bi % 2
                    for hh in range(4):
                        if hh < 2:
                            P.add("act", (lambda h, hh=hh: h.activation(
                                out=pn[s][:, hh, :], in_=pp[s][:, hh, :], func=AF.Copy, scale=st[s][:, 20 + hh:21 + hh])),
                                R=[b_pp[s], b_stD[s]], W=[b_pn[s][0]], acc=(hh > 0))
                        else:
                            P.add("dve", (lambda h, hh=hh: h.tensor_scalar(
                                out=pn[s][:, hh, :], in0=pp[s][:, hh, :], scalar1=st[s][:, 20 + hh:21 + hh], scalar2=None,
                                op0=ALU.mult)), R=[b_pp[s], b_stD[s]], W=[b_pn[s][1]], acc=(hh > 2))

                def E(bi):
                    s = bi % 2
                    for hh in range(4):
                        bk_ = hh // 2
                        for c in range(3):
                            off = (hh % 2) * 384 + c * 128
                            P.add("pe", (lambda h, hh=hh, c=c, bk_=bk_, off=off: h.transpose(
                                ps_t[:, bk_, off:off + 128], pn[s][:, hh, c * 128:(c + 1) * 128], ident[:])),
                                R=[b_pn[s][hh // 2], b_ident], W=[b_pst[bk_]], acc=(not (hh % 2 == 0 and c == 0)))

                def F(bi):
                    s = bi % 2
                    P.add("act", (lambda h: h.activation(out=pT[s][:, 0:2, :], in_=ps_t[:, 0, 0:768].rearrange(
                        "p (a b) -> p a b", a=2), func=AF.Copy)), R=[b_pst[0]], W=[b_pT[s][0]])
                    P.add("dve", (lambda h: h.tensor_copy(out=pT[s][:, 2:4, :], in_=ps_t[:, 1, 0:768].rearrange(
                        "p (a b) -> p a b", a=2))), R=[b_pst[1]], W=[b_pT[s][1]])

                def G(bi):
                    qb, g = blocks[bi]
                    s = bi % 2
                    for hh in range(4):
                        for c in range(3):
                            P.mm(ps_o[:, hh, :], vv_[:, qb + c, g * 128:(g + 1) * 128], pT[s][:, hh, c * 128:(c + 1) * 128],
                                 (c == 0), (c == 2), R=[bv, b_pT[s][hh // 2]], W=[b_pso], acc=(not (hh == 0 and c == 0)))

                def H(bi):
                    qb, g = blocks[bi]
                    P.add("act", (lambda h: h.activation(
                        out=o[:, g * 4:(g + 1) * 4, qb * 128:(qb + 1) * 128], in_=ps_o[:], func=AF.Copy)),
                        R=[b_pso], W=[bo])

                NBK = len(blocks)
                nparts = norm_parts(i + 1) if i + 1 < ntiles else []
                if i + 1 < ntiles:
                    normed.add(i + 1)
                if i + 2 < ntiles:
                    load(i + 2)
                A(0)
                B1(0)
                B2(0)
                if NBK > 1:
                    A(1)
                C(0)
                if NBK > 1:
                    B1(1)
                for bi in range(NBK):
                    if bi + 1 < NBK:
                        B2(bi + 1)
                    if bi >= 1:
                        F(bi - 1)
                        G(bi - 1)
                    if bi + 2 < NBK:
                        A(bi + 2)
                    if bi + 1 < NBK:
                        C(bi + 1)
                    Dd(bi)
                    D2(bi)
                    if bi + 2 < NBK:
                        B1(bi + 2)
                    if bi >= 1:
                        H(bi - 1)
                    E(bi)
                    if bi < len(nparts):
                        nparts[bi]()
                F(NBK - 1)
                G(NBK - 1)
                H(NBK - 1)
                while nxt:
                    nxt.popleft()()
                P.add("sp", lambda h: h.dma_start(out=self.dview(self.ot_s, i * T, T), in_=o[:]), R=[bo], dma=True)

            load(0)
            if ntiles > 1:
                load(1)
            for i in range(ntiles):
                g0 = proj_groups(i)
                while g0:
                    g0.popleft()()
                tile(i, deque())
            P.barrier()

    def phase5b(self):
        nc, P = self.nc, self.P
        ntiles = self.n512
        with ExitStack() as es:
            wg2 = es.enter_context(self._sbt("wg2", [128, KC, 2048], BF16))
            wao = es.enter_context(self._sbt("wao", [128, KC, D], BF16))
            wro = es.enter_context(self._sbt("wro", [128, KC, D], BF16))
            wout = es.enter_context(self._sbt("wout", [128, KC, D], BF16))
            xts = [es.enter_context(self._sbt("xt%d" % i, [128, KC, T], F32)) for i in range(2)]
            ots = [es.enter_context(self._sbt("ot%d" % i, [128, KC, T], BF16)) for i in range(2)]
            hys = [es.enter_context(self._sbt("hy%d" % i, [128, KC, T], BF16)) for i in range(2)]
            ht = es.enter_context(self._sbt("ht", [128, KC, T], BF16))
            rstd = es.enter_context(self._sbt("rstd", [128, T], F32))
            mg = es.enter_context(self._sbt("mg", [128, KC, T], BF16))
            sg = [es.enter_context(self._sbt("sg%d" % i, [128, T], F32)) for i in range(2)]
            mA = [es.enter_context(self._sbt("mA%d" % i, [128, T], F32)) for i in range(2)]
            mB = [es.enter_context(self._sbt("mB%d" % i, [128, T], F32)) for i in range(2)]
            psum = es.enter_context(self._pst("ps5", [128, 8, T], F32))
            b_xt = [Buf("xt0"), Buf("xt1")]
            b_ot = [Buf("ot0"), Buf("ot1")]
            b_hy = [Buf("hy0"), Buf("hy1")]
            b_ht, b_rstd = Buf("ht"), Buf("rstd")
            b_mg = [Buf("mg%d" % k) for k in range(KC)]
            b_sg = [Buf("sg0"), Buf("sg1")]
            b_mA, b_mB = [Buf("mA0"), Buf("mA1")], [Buf("mB0"), Buf("mB1")]
            b_ps = [Buf("ps%d" % i) for i in range(8)]
            b_wg2 = self.load_w(P, wg2, "wg2", self.w["w_in"], KC, 2048, c0=3584)
            b_wao = self.load_w(P, wao, "wao", self.w["w_attn_o"], KC, D)
            b_wro = self.load_w(P, wro, "wro", self.w["w_rnn_o"], KC, D)
            b_wout = self.load_w(P, wout, "wout", self.w["w_out"], KC, D)

            def load(i):
                s = i % 2
                P.add("sp", lambda h: h.dma_start(out=xts[s][:], in_=self.dview(self.x1_s, PAD + i * T, T)),
                      W=[b_xt[s]], dma=True)
                P.add("sp", lambda h: h.dma_start(out=ots[s][:], in_=self.dview(self.ot_s, i * T, T)), W=[b_ot[s]], dma=True)
                P.add("sp", lambda h: h.dma_start(out=hys[s][:], in_=self.dview(self.hy_s, i * T, T)), W=[b_hy[s]], dma=True)

            def norm(i, stage=None):
                s = i % 2
                self.norm_full(P, xts[s], b_xt[s], T, PC_GM, ht, b_ht, [(psum[:, 6, :], b_ps[6])],
                               rstd, b_rstd, ht, b_ht, stage=stage)

            step = [0]

            def tile(i):
                s = i % 2
                for m in range(KC):
                    u = m % 2
                    for br in range(2):
                        pr = 2 * (step[0] % 2)
                        step[0] += 1
                        if br == 0:
                            wt, bw, src, bsrc, gcol = wao, b_wao, ots[s], b_ot[s], m * 128
                        else:
                            wt, bw, src, bsrc, gcol = wro, b_wro, hys[s], b_hy[s], 1024 + m * 128
                        for k in range(KC):
                            P.mm(psum[:, pr, :], wt[:, k, m * 128:(m + 1) * 128], src[:, k, :], (k == 0), (k == KC - 1),
                                 R=bw[k] + [bsrc], W=[b_ps[pr]])
                        for k in range(KC):
                            P.mm(psum[:, pr + 1, :], wg2[:, k, gcol:gcol + 128], ht[:, k, :], (k == 0), (k == KC - 1),
                                 R=b_wg2[k] + [b_ht], W=[b_ps[pr + 1]])
                        P.add("act", (lambda h, br=br, pr=pr: h.activation(out=sg[br][:], in_=psum[:, pr + 1, :], func=AF.Sigmoid)),
                              R=[b_ps[pr + 1]], W=[b_sg[br]])
                        mo, bmo = (mA[u], b_mA[u]) if br == 0 else (mB[u], b_mB[u])
                        P.add("dve", (lambda h, br=br, pr=pr, mo=mo: h.tensor_tensor(
                            out=mo[:], in0=sg[br][:], in1=psum[:, pr, :], op=ALU.mult)),
                            R=[b_sg[br], b_ps[pr]], W=[bmo])
                    P.add("dve", (lambda h, u=u, m=m: h.tensor_tensor(out=mg[:, m, :], in0=mA[u][:], in1=mB[u][:], op=ALU.add)),
                          R=[b_mA[u], b_mB[u]], W=[b_mg[m]])
                if i + 1 < ntiles:
                    norm(i + 1, "sq")
                for m in range(KC):
                    pb = 4 + (m % 2)
                    for k in range(KC):
                        P.mm(psum[:, pb, :], wout[:, k, m * 128:(m + 1) * 128], mg[:, k, :], (k == 0), (k == KC - 1),
                             R=b_wout[k] + [b_mg[k]], W=[b_ps[pb]])
                    P.add("dve", (lambda h, pb=pb, m=m: h.tensor_tensor(
                        out=xts[s][:, m, :], in0=psum[:, pb, :], in1=xts[s][:, m, :], op=ALU.add)),
                        R=[b_ps[pb], b_xt[s]], W=[b_xt[s]])
                    if i + 1 < ntiles:
                        if m == 2:
                            norm(i + 1, "ms")
                        if m >= 4:
                            norm(i + 1, ("h", 2 * (m - 4), 2 * (m - 4) + 2))
                P.add("sp", lambda h: h.dma_start(out=self.dview(self.x2_s, i * T, T), in_=xts[s][:]),
                      R=[b_xt[s]], dma=True)
                if i + 2 < ntiles:
                    load(i + 2)

            load(0)
            if ntiles > 1:
                load(1)
            norm(0)
            for i in range(ntiles):
                tile(i)
            P.barrier()


def _pk(v):
    return np.ascontiguousarray(np.asarray(v, np.float32).reshape(8, 128).T)


def make_consts():
    cst = np.zeros((128, 512), np.float32)
    cst[:, 0:128] = np.eye(128, dtype=np.float32)
    q = np.arange(128)[:, None]
    c = np.arange(384)[None, :]
    dist = np.abs(q + 128 - c).astype(np.float32)
    cst[:, 128:512] = np.where(dist <= 128, -dist, -1e30).astype(np.float32)
    return cst


def make_par(inp, seq_starts, NT):
    n512 = NT // T
    par = np.zeros((128, PC_FLAG + 2 * n512), np.float32)
    par[:, PC_G1:PC_G1 + 8] = _pk(inp["norm_ffn1"][0])
    par[:, PC_GM:PC_GM + 8] = _pk(inp["norm_mix"][0])
    par[:, PC_G2:PC_G2 + 8] = _pk(inp["norm_ffn2"][0])
    par[:, PC_GF:PC_GF + 8] = _pk(inp["norm_final"])
    for tap in range(4):
        par[:, PC_CW + tap * 8:PC_CW + tap * 8 + 8] = _pk(inp["conv_w"][0, tap])
    par[:, PC_CB:PC_CB + 8] = _pk(inp["conv_b"][0])
    par[:, PC_BAF:PC_BAF + 8] = _pk(inp["lru_ba_f"][0])
    par[:, PC_BIF:PC_BIF + 8] = _pk(inp["lru_bi_f"][0])
    par[:, PC_LAMF:PC_LAMF + 8] = _pk(inp["lru_lam_f"][0])
    par[:, PC_BAB:PC_BAB + 8] = _pk(inp["lru_ba_b"][0])
    par[:, PC_BIB:PC_BIB + 8] = _pk(inp["lru_bi_b"][0])
    par[:, PC_LAMB:PC_LAMB + 8] = _pk(inp["lru_lam_b"][0])
    par[:, PC_SINK:PC_SINK + 8] = np.asarray(inp["attn_sink"], np.float32).reshape(1, 8)
    par[:, PC_SLOPE:PC_SLOPE + 8] = (2.0 ** (-np.arange(1, 9, dtype=np.float64))).astype(np.float32)[None, :]
    starts = set(int(s) // T for s in seq_starts)
    for i in range(n512):
        par[:, PC_FLAG + i] = 0.0 if i in starts else 1.0
        par[:, PC_FLAG + n512 + i] = 0.0 if ((i + 1) in starts or i + 1 == n512) else 1.0
    return par


def weight_map(inp):
    f = lambda a: np.ascontiguousarray(np.asarray(a, np.float32))
    return {
        "ffn1_gate": f(inp["ffn1_gate"][0]), "ffn1_up": f(inp["ffn1_up"][0]), "ffn1_down": f(inp["ffn1_down"][0]),
        "w_in": f(inp["w_in"][0]),
        "lru_wa_f": f(inp["lru_wa_f"][0]).reshape(1024, 128), "lru_wi_f": f(inp["lru_wi_f"][0]).reshape(1024, 128),
        "lru_wa_b": f(inp["lru_wa_b"][0]).reshape(1024, 128), "lru_wi_b": f(inp["lru_wi_b"][0]).reshape(1024, 128),
        "w_attn_o": f(inp["w_attn_o"][0]), "w_rnn_o": f(inp["w_rnn_o"][0]), "w_out": f(inp["w_out"][0]),
        "ffn2_gate": f(inp["ffn2_gate"][0]), "ffn2_up": f(inp["ffn2_up"][0]), "ffn2_down": f(inp["ffn2_down"][0]),
    }


_NC_CACHE = {}


def kernel(**inp):
    NT = NT_FULL
    xp = np.asarray(inp["x_prompt"], np.float32)
    xs = np.asarray(inp["x_sample"], np.float32)
    S = xs.shape[1]
    counts = [6, 6, 5, 5, 5, 5]
    assign = []
    b = 0
    for c in counts:
        assign.append(list(range(b, b + c)))
        b += c
    wm = weight_map(inp)
    cst = make_consts()
    in_maps = []
    for core in range(NCORES):
        xT = np.zeros((D, NT), np.float32)
        if core < 2:
            xT[:, :] = xp[core].T
            starts = [0]
        else:
            ids = assign[core - 2]
            for j, sid in enumerate(ids):
                xT[:, j * S:(j + 1) * S] = xs[sid].T
            starts = [j * S for j in range(NT // S)]
        m = {"xT": xT, "par": make_par(inp, starts, NT), "cst": cst}
        m.update(wm)
        in_maps.append(m)
    if NT not in _NC_CACHE:
        _NC_CACHE[NT] = Builder(NT).build()
    nc = _NC_CACHE[NT]
    res = run_bass_kernel_spmd(nc, in_maps, core_ids=list(range(NCORES)))
    y_prompt = np.empty_like(xp)
    y_sample = np.empty_like(xs)
    for core in range(NCORES):
        yT = res.results[core]["yT"]
        if core < 2:
            y_prompt[core] = yT.T
        else:
            ids = assign[core - 2]
            for j, sid in enumerate(ids):
                y_sample[sid] = yT[:, j * S:(j + 1) * S].T
    return (y_prompt, y_sample)
```
